# Optimizing a Trainium2 kernel written in Bass

```python
import jax, jax.numpy as jnp
from jax import lax
import numpy as np

D_MODEL = 1024
BATCH = 16
SEQ = 2048
DEPTH = 1

CTX_LEN = 256
GRID_W = 64
NA_HEADS = 8
HEAD_DIM = 64
NA_WIDTH = NA_HEADS * HEAD_DIM
NA_WIN_ROWS = 8
NA_WIN_COLS = 16
GM_GROUPS = 8
GM_CHUNK = 128
GM_WIDTH = D_MODEL // 2
GM_GROUP_DIM = GM_WIDTH // GM_GROUPS
PEER_HEADS = 8
PEER_N_KEYS = 128
PEER_EXPERTS = PEER_N_KEYS * PEER_N_KEYS
PEER_TOPK = 16
PEER_QDIM = 256
PEER_HALF = PEER_QDIM // 2
PEER_BLOCK = 128
ADA_CHUNKS = 6
EPS = 1e-6
NEG_INF = -1e30
IN_SPLITS = (NA_WIDTH, 2 * NA_WIDTH, 3 * NA_WIDTH, 3 * NA_WIDTH + GM_WIDTH,
             3 * NA_WIDTH + 2 * GM_WIDTH, 3 * NA_WIDTH + 2 * GM_WIDTH + D_MODEL)
IN_COLS = 3 * NA_WIDTH + 2 * GM_WIDTH + 2 * D_MODEL

kernel_name = 'hybrid_natten_sgmlp_peer_block'


def rmsnorm(x, g):
    xf = x.astype(jnp.float32)
    y = xf * lax.rsqrt(jnp.mean(xf * xf, axis=-1, keepdims=True) + EPS)
    return (y * g.astype(jnp.float32)).astype(x.dtype)


def layernorm(x, g):
    xf = x.astype(jnp.float32)
    xc = xf - jnp.mean(xf, axis=-1, keepdims=True)
    y = xc * lax.rsqrt(jnp.mean(xc * xc, axis=-1, keepdims=True) + EPS)
    return (y * g.astype(jnp.float32)).astype(x.dtype)


def modulate(h, shift, scale):
    return h * (1 + scale) + shift


def heads(z):
    return z.reshape(z.shape[:-1] + (NA_HEADS, HEAD_DIM))


def neighborhood_attention(q, k, v, k_ctx, v_ctx, rpb):
    B, S = q.shape[0], q.shape[1]
    rows = S // GRID_W
    kh = min(NA_WIN_ROWS, rows)
    band = kh * GRID_W
    scale = HEAD_DIM ** -0.5
    qg = q.reshape(B, rows, GRID_W, NA_HEADS, HEAD_DIM)
    kg = k.reshape(B, rows, GRID_W, NA_HEADS, HEAD_DIM)
    vg = v.reshape(B, rows, GRID_W, NA_HEADS, HEAD_DIM)
    col = jnp.arange(GRID_W)
    col_start = jnp.clip(col - NA_WIN_COLS // 2, 0, GRID_W - NA_WIN_COLS)
    col_in = (col[None, :] >= col_start[:, None]) & (col[None, :] < col_start[:, None] + NA_WIN_COLS)
    band_mask = jnp.broadcast_to(col_in[:, None, :], (GRID_W, kh, GRID_W)).reshape(GRID_W, band)
    dc_idx = jnp.clip(col[None, :] - col[:, None] + NA_WIN_COLS - 1, 0, 2 * NA_WIN_COLS - 2)

    def row_step(r):
        rs = jnp.clip(r - NA_WIN_ROWS // 2, 0, rows - kh)
        q_r = lax.dynamic_index_in_dim(qg, r, axis=1, keepdims=False)
        k_b = lax.dynamic_slice_in_dim(kg, rs, kh, axis=1).reshape(B, band, NA_HEADS, HEAD_DIM)
        v_b = lax.dynamic_slice_in_dim(vg, rs, kh, axis=1).reshape(B, band, NA_HEADS, HEAD_DIM)
        dr_idx = rs + jnp.arange(kh) - r + NA_WIN_ROWS - 1
        bias = rpb[:, dr_idx][:, :, dc_idx]
        bias = bias.transpose(0, 2, 1, 3).reshape(NA_HEADS, GRID_W, band).astype(jnp.float32)
        s_loc = jnp.einsum('bqhd,bkhd->bhqk', q_r, k_b).astype(jnp.float32) * scale + bias
        s_loc = jnp.where(band_mask, s_loc, NEG_INF)
        s_ctx = jnp.einsum('bqhd,bchd->bhqc', q_r, k_ctx).astype(jnp.float32) * scale
        p = jax.nn.softmax(jnp.concatenate([s_loc, s_ctx], axis=-1), axis=-1).astype(v.dtype)
        return (jnp.einsum('bhqk,bkhd->bqhd', p[..., :band], v_b)
                + jnp.einsum('bhqc,bchd->bqhd', p[..., band:], v_ctx))

    o = lax.map(row_step, jnp.arange(rows))
    return jnp.moveaxis(o, 0, 1).reshape(B, S, NA_WIDTH)


def context_attention(q, k, v):
    s = jnp.einsum('bqhd,bkhd->bhqk', q, k).astype(jnp.float32) * (HEAD_DIM ** -0.5)
    p = jax.nn.softmax(s, axis=-1).astype(v.dtype)
    o = jnp.einsum('bhqk,bkhd->bqhd', p, v)
    return o.reshape(o.shape[0], o.shape[1], NA_WIDTH)


def spatial_gating(gu, gv, ln_g, ws, bs):
    B, L, _ = gv.shape
    n = L // GM_CHUNK
    u = jax.nn.gelu(gu)
    vn = layernorm(jax.nn.gelu(gv), ln_g)
    vg = vn.reshape(B, n, GM_CHUNK, GM_GROUPS, GM_GROUP_DIM)
    mixed = jnp.einsum('gpq,bnqgd->bnpgd', ws, vg) + bs.T[:, :, None]
    return u * mixed.reshape(B, L, GM_WIDTH)


def merge_branches(y_a, y_b, ga, gb, w_pa, w_pb, w_out):
    m = jax.nn.sigmoid(ga) * (y_a @ w_pa) + jax.nn.sigmoid(gb) * (y_b @ w_pb)
    return m @ w_out


def peer_ffn(h, wq, sub_keys, expert_u, expert_v):
    B, L, D = h.shape
    blocks = h.reshape(B * L // PEER_BLOCK, PEER_BLOCK, D)

    def block(xb):
        q = (xb @ wq).reshape(PEER_BLOCK, PEER_HEADS, 2, PEER_HALF)
        s = jnp.einsum('thpd,hpkd->thpk', q, sub_keys).astype(jnp.float32)
        s_top, i_top = lax.top_k(s, PEER_TOPK)
        cand_s = (s_top[:, :, 0, :, None] + s_top[:, :, 1, None, :]).reshape(PEER_BLOCK, PEER_HEADS, PEER_TOPK * PEER_TOPK)
        cand_i = (i_top[:, :, 0, :, None] * PEER_N_KEYS + i_top[:, :, 1, None, :]).reshape(PEER_BLOCK, PEER_HEADS, PEER_TOPK * PEER_TOPK)
        best_s, best_pos = lax.top_k(cand_s, PEER_TOPK)
        experts = jnp.take_along_axis(cand_i, best_pos, axis=-1)
        g = jax.nn.softmax(best_s, axis=-1)
        act = jax.nn.gelu(jnp.einsum('thkd,td->thk', expert_u[experts], xb).astype(jnp.float32))
        wgt = (g * act).astype(xb.dtype)
        return jnp.einsum('thk,thkd->td', wgt, expert_v[experts])

    return lax.map(block, blocks).reshape(B, L, D)


def setup_inputs(seed: int = 0) -> dict:
    key = jax.random.key(seed)
    ks = jax.random.split(key, 21)
    nrm = jax.random.normal
    return {
        'x': nrm(ks[0], (BATCH, SEQ, D_MODEL), jnp.float32),
        'c': nrm(ks[1], (BATCH, D_MODEL), jnp.float32),
        'ctx': nrm(ks[2], (BATCH, CTX_LEN, D_MODEL), jnp.float32),
        'c_ctx': nrm(ks[3], (D_MODEL,), jnp.float32),
        'ada_w': nrm(ks[4], (DEPTH, D_MODEL, ADA_CHUNKS * D_MODEL), jnp.float32) * (0.3 * D_MODEL ** -0.5),
        'ada_b': nrm(ks[5], (DEPTH, ADA_CHUNKS * D_MODEL), jnp.float32) * 0.02,
        'norm1_g': 1.0 + 0.01 * nrm(ks[6], (DEPTH, D_MODEL), jnp.float32),
        'norm2_g': 1.0 + 0.01 * nrm(ks[7], (DEPTH, D_MODEL), jnp.float32),
        'w_in': nrm(ks[8], (DEPTH, D_MODEL, IN_COLS), jnp.float32) * D_MODEL ** -0.5,
        'na_rpb': nrm(ks[9], (DEPTH, NA_HEADS, 2 * NA_WIN_ROWS - 1, 2 * NA_WIN_COLS - 1), jnp.float32) * 0.1,
        'gm_ln_g': 1.0 + 0.01 * nrm(ks[10], (DEPTH, GM_WIDTH), jnp.float32),
        'gm_ws': nrm(ks[11], (DEPTH, GM_GROUPS, GM_CHUNK, GM_CHUNK), jnp.float32) * GM_CHUNK ** -0.5,
        'gm_bs': 1.0 + 0.02 * nrm(ks[12], (DEPTH, GM_GROUPS, GM_CHUNK), jnp.float32),
        'w_proj_a': nrm(ks[13], (DEPTH, NA_WIDTH, D_MODEL), jnp.float32) * NA_WIDTH ** -0.5,
        'w_proj_b': nrm(ks[14], (DEPTH, GM_WIDTH, D_MODEL), jnp.float32) * GM_WIDTH ** -0.5,
        'w_out': nrm(ks[15], (DEPTH, D_MODEL, D_MODEL), jnp.float32) * D_MODEL ** -0.5,
        'peer_wq': nrm(ks[16], (DEPTH, D_MODEL, PEER_HEADS * PEER_QDIM), jnp.float32) * D_MODEL ** -0.5,
        'peer_keys': nrm(ks[17], (DEPTH, PEER_HEADS, 2, PEER_N_KEYS, PEER_HALF), jnp.float32) * PEER_HALF ** -0.5,
        'peer_u': nrm(ks[18], (DEPTH, PEER_EXPERTS, D_MODEL), jnp.float32) * D_MODEL ** -0.5,
        'peer_v': nrm(ks[19], (DEPTH, PEER_EXPERTS, D_MODEL), jnp.float32) * 0.5,
        'final_g': 1.0 + 0.01 * nrm(ks[20], (D_MODEL,), jnp.float32),
    }


def reference(x, c, ctx, c_ctx, ada_w, ada_b, norm1_g, norm2_g, w_in, na_rpb, gm_ln_g, gm_ws, gm_bs,
              w_proj_a, w_proj_b, w_out, peer_wq, peer_keys, peer_u, peer_v, final_g):
    for layer in range(DEPTH):
        last = layer == DEPTH - 1
        sh1, sc1, g1, sh2, sc2, g2 = jnp.split(
            (jax.nn.silu(c) @ ada_w[layer] + ada_b[layer])[:, None, :], ADA_CHUNKS, axis=-1)
        csh1, csc1, cg1, csh2, csc2, cg2 = jnp.split(
            jax.nn.silu(c_ctx) @ ada_w[layer] + ada_b[layer], ADA_CHUNKS, axis=-1)
        w = w_in[layer]
        h = modulate(rmsnorm(x, norm1_g[layer]), sh1, sc1)
        hc = modulate(rmsnorm(ctx, norm1_g[layer]), csh1, csc1)
        q, k, v, gu, gv, ga, gb = jnp.split(h @ w, IN_SPLITS, axis=-1)
        if last:
            k_c, v_c = jnp.split(hc @ w[:, NA_WIDTH:3 * NA_WIDTH], 2, axis=-1)
        else:
            qc, k_c, v_c, guc, gvc, gac, gbc = jnp.split(hc @ w, IN_SPLITS, axis=-1)
        y_a = neighborhood_attention(heads(q), heads(k), heads(v), heads(k_c), heads(v_c), na_rpb[layer])
        y_b = spatial_gating(gu, gv, gm_ln_g[layer], gm_ws[layer], gm_bs[layer])
        x = x + g1 * merge_branches(y_a, y_b, ga, gb, w_proj_a[layer], w_proj_b[layer], w_out[layer])
        h2 = modulate(rmsnorm(x, norm2_g[layer]), sh2, sc2)
        x = x + g2 * peer_ffn(h2, peer_wq[layer], peer_keys[layer], peer_u[layer], peer_v[layer])
        if not last:
            yc_a = context_attention(heads(qc), heads(k_c), heads(v_c))
            yc_b = spatial_gating(guc, gvc, gm_ln_g[layer], gm_ws[layer], gm_bs[layer])
            ctx = ctx + cg1 * merge_branches(yc_a, yc_b, gac, gbc, w_proj_a[layer], w_proj_b[layer], w_out[layer])
            hc2 = modulate(rmsnorm(ctx, norm2_g[layer]), csh2, csc2)
            ctx = ctx + cg2 * peer_ffn(hc2, peer_wq[layer], peer_keys[layer], peer_u[layer], peer_v[layer])
    return rmsnorm(x, final_g)
```

```python
import numpy as np
from contextlib import ExitStack
import concourse.bass as bass
import concourse.mybir as mybir
from concourse.bass_utils import run_bass_kernel_spmd

F32 = mybir.dt.float32
BF16 = mybir.dt.bfloat16
I32 = mybir.dt.int32
U32 = mybir.dt.uint32
AF = mybir.ActivationFunctionType
ALU = mybir.AluOpType
AX = mybir.AxisListType
EPS = 1e-6
NEG = -30000.0
NCORES = 8
SEQ = 2048
D = 1024
NB = 8


class Res:
    __slots__ = ("w", "r", "dsem", "dcnt", "name", "bg")

    def __init__(self, name="", bg=False):
        self.bg = bg
        self.w = None
        self.r = {}
        self.dsem = None
        self.dcnt = 0
        self.name = name


class K:
    def __init__(self, nc, stack):
        self.nc = nc
        self.stack = stack
        self.eng = {"pe": nc.tensor, "act": nc.scalar, "dve": nc.vector, "pool": nc.gpsimd, "sp": nc.sync}
        self.esem = {k: stack.enter_context(nc.semaphore("es_" + k)) for k in self.eng}
        self.ecnt = {k: 0 for k in self.eng}
        self.waited = {k: {} for k in self.eng}
        self.dres = []
        self.nsem = 0

    def _wait(self, e, tok, kind):
        if tok is None:
            return
        sem, val = tok
        if sem is self.esem[e]:
            if e == "pe" or kind != "raw":
                return
        if self.waited[e].get(sem, 0) >= val:
            return
        self.eng[e].wait_ge(sem, val)
        self.waited[e][sem] = val

    def _pre(self, e, reads, writes):
        for r in reads:
            self._wait(e, r.w, "raw")
        for w in writes:
            self._wait(e, w.w, "waw")
            for sem, val in list(w.r.items()):
                self._wait(e, (sem, val), "war")

    def _post(self, tok, reads, writes):
        for r in reads:
            if r.r.get(tok[0], 0) < tok[1]:
                r.r[tok[0]] = tok[1]
        for w in writes:
            w.w = tok
            w.r = {}

    def op(self, e, fn, reads=(), writes=()):
        self._pre(e, reads, writes)
        ins = fn(self.eng[e])
        self.ecnt[e] += 1
        ins.then_inc(self.esem[e], 1)
        tok = (self.esem[e], self.ecnt[e])
        self._post(tok, reads, writes)
        return tok

    def dma(self, e, fn, owner, reads=(), writes=()):
        self._pre(e, reads, writes)
        if owner.dsem is None:
            owner.dsem = self.stack.enter_context(self.nc.semaphore("ds%d" % self.nsem))
            self.nsem += 1
            self.dres.append(owner)
        elif owner.dcnt > 0:
            self._wait(e, (owner.dsem, owner.dcnt), "raw")
        ins = fn(self.eng[e])
        owner.dcnt += 16
        ins.then_inc(owner.dsem, 16)
        tok = (owner.dsem, owner.dcnt)
        self._post(tok, reads, writes)
        return tok

    def barrier(self, full=False):
        for e in self.eng:
            for o in self.eng:
                if o != e and self.ecnt[o] > 0:
                    self._wait(e, (self.esem[o], self.ecnt[o]), "raw")
            for r in self.dres:
                if r.dcnt > 0 and (full or not r.bg):
                    self._wait(e, (r.dsem, r.dcnt), "raw")


def _chunk_keys():
    rows = SEQ // 64
    types = {}
    lists = []
    for rp in range(rows // 2):
        qrows = (2 * rp, 2 * rp + 1)
        rs = [min(max(r - 4, 0), rows - 8) for r in qrows]
        kc_lo = rs[0] // 2
        kc_hi = (rs[1] + 7) // 2
        lst = []
        for kc in range(kc_lo, kc_hi + 1):
            pat = tuple(tuple(rs[b] <= 2 * kc + a <= rs[b] + 7 for b in range(2)) for a in range(2))
            key = (kc - rp, pat)
            if key not in types:
                types[key] = len(types)
            lst.append((kc, types[key]))
        lists.append(lst)
    return lists, types


CHUNK_KEYS, TILE_TYPES = _chunk_keys()
NTT = len(TILE_TYPES)


def _bias_tiles(rpb):
    qcol = np.arange(64)
    kcol = np.arange(64)
    cs = np.clip(qcol - 8, 0, 48)
    col_in = (kcol[None, :] >= cs[:, None]) & (kcol[None, :] < cs[:, None] + 16)
    dc = np.clip(kcol[None, :] - qcol[:, None] + 15, 0, 30)
    out = np.full((128, 8 * NTT, 128), NEG, np.float32)
    for (o, pat), ti in TILE_TYPES.items():
        for a in range(2):
            for b in range(2):
                if not pat[a][b]:
                    continue
                dr = 2 * o + a - b
                assert -7 <= dr <= 7
                for h in range(8):
                    blk = np.where(col_in, rpb[h, dr + 7][dc], np.float32(NEG)).astype(np.float32)
                    out[b * 64:(b + 1) * 64, h * NTT + ti, a * 64:(a + 1) * 64] = blk
    return out


def build_program():
    nc = bass.Bass("TRN2", target_bir_lowering=False)
    dd = {}

    def din(name, shape, dt=F32):
        dd[name] = nc.dram_tensor(name, list(shape), dt, kind="ExternalInput").ap()
        return dd[name]

    x_d = din("x", [2 * SEQ, D])
    ctx_d = din("ctx", [512, D])
    cT_d = din("cT", [128, 8, 4])
    adaw_d = din("ada_w", [D, 6 * D])
    adab_d = din("adab_rep", [4, 6 * D])
    n1gT_d = din("n1gT", [128, 8])
    n2g_d = din("n2g_rep", [128, D])
    fg_d = din("fg_rep", [128, D])
    lng_d = din("lng_rep", [128, 512])
    win_d = din("w_in", [D, 4608])
    bias_d = din("bias_t", [128, 8 * NTT, 128])
    wsT_d = din("wsT", [128, 8, 128])
    bsT_d = din("bsT", [128, 8])
    wpa_d = din("w_pa", [512, D])
    wpb_d = din("w_pb", [512, D])
    wout_d = din("w_out", [D, D])
    wq_d = din("peer_wq", [D, 2048])
    keysT_d = din("keysT", [128, 16, 128])
    uv_d = din("peer_uv", [16384, 2 * D])
    ident_d = din("ident", [128, 128])
    iota_d = din("iota16", [128, 16])
    sel_d = din("sel", [4, 2, 128])
    out_d = nc.dram_tensor("out", [2 * SEQ, D], F32, kind="ExternalOutput").ap()

    win_v = win_d.rearrange("(kb p) c -> p kb c", p=128)
    adaw_v = adaw_d.rearrange("(kb p) c -> p kb c", p=128)

    with ExitStack() as outer:
        k = K(nc, outer)
        uid = [0]

        def sb(stack, shape, dt, nm="t"):
            uid[0] += 1
            return stack.enter_context(nc.sbuf_tensor("%s%d" % (nm, uid[0]), list(shape), dt))

        def ps(stack, nm):
            return stack.enter_context(nc.psum_tensor(nm, [128, 1024], F32))

        PA, PB, PC, PD = [ps(outer, "ps%d" % i) for i in range(4)]
        rPA, rPB, rPC, rPD = Res("PA"), Res("PB"), Res("PC"), Res("PD")

        def v8(p):
            return p[:].rearrange("p (b t) -> p b t", t=128)

        ident = sb(outer, [128, 128], F32, "ident")
        identb = sb(outer, [128, 128], BF16, "identb")
        tmp4 = sb(outer, [4, D], F32, "tmp4")
        rtmp4 = Res("tmp4")
        rowsd = nc.dram_tensor("rowsd", [4, 6 * D], F32).ap()
        rrowsd = Res("rowsd")
        A1T = sb(outer, [128, 8, 4], F32, "A1T")
        modT = sb(outer, [128, 16, 4], F32, "modT")
        n1gT = sb(outer, [128, 8], F32, "n1gT")
        sel = sb(outer, [4, 2, 128], F32, "sel")
        epsb = sb(outer, [128, 1], F32, "epsb")
        rconst = Res("const")
        rrows = Res("rows")
        rmod = Res("mod")

        def ld(dst_ap, src_ap, res, eng="sp"):
            return k.dma(eng, lambda e: e.dma_start(out=dst_ap, in_=src_ap), res, writes=[res])

        ld(ident[:], ident_d[:, :], rconst)
        ld(n1gT[:], n1gT_d[:, :], rconst)
        ld(sel[:], sel_d[:, :, :], rconst)
        k.op("pool", lambda e: e.tensor_copy(out=identb[:], in_=ident[:]), reads=[rconst], writes=[rconst])
        k.op("pool", lambda e: e.memset(epsb[:], EPS), writes=[rconst])

        with ExitStack() as s0:
            rows = sb(s0, [4, 6 * D], F32, "rows")
            cT = sb(s0, [128, 8, 4], F32, "cT")
            scT = sb(s0, [128, 8, 4], F32, "scT")
            adab = sb(s0, [4, 6 * D], F32, "adab")
            wbuf = [sb(s0, [128, 8, 1024], F32, "wbuf") for _ in range(2)]
            rwbuf = [Res("wbuf0"), Res("wbuf1")]
            rc = Res("cT")
            rab = Res("adab")
            ld(cT[:], cT_d[:, :, :], rc)
            ld(adab[:], adab_d[:, :], rab)
            k.op("act", lambda e: e.activation(out=scT[:], in_=cT[:], func=AF.Silu), reads=[rc], writes=[rc])
            for pc in range(6):
                wb, rw = wbuf[pc % 2], rwbuf[pc % 2]
                ld(wb[:], adaw_v[:, :, pc * 1024:(pc + 1) * 1024], rw)
                for half in range(2):
                    for kb in range(8):
                        k.op("pe", lambda e, kb=kb, half=half, wb=wb: e.matmul(
                            PA[0:4, half * 512:(half + 1) * 512], lhsT=scT[:, kb, :],
                            rhs=wb[:, kb, half * 512:(half + 1) * 512], start=(kb == 0), stop=(kb == 7)),
                            reads=[rc, rw], writes=[rPA])
                k.op("dve", lambda e, pc=pc: e.tensor_tensor(
                    out=rows[0:4, pc * 1024:(pc + 1) * 1024], in0=PA[0:4, :],
                    in1=adab[0:4, pc * 1024:(pc + 1) * 1024], op=ALU.add),
                    reads=[rPA, rab], writes=[rrows])
            k.dma("sp", lambda e: e.dma_start(out=rowsd[:, :], in_=rows[0:4, :]), rrowsd, reads=[rrows], writes=[rrowsd])
            for blk in range(16):
                k.op("pe", lambda e, blk=blk: e.transpose(
                    out=PB[:, blk * 4:(blk + 1) * 4], in_=rows[0:4, blk * 128:(blk + 1) * 128],
                    identity=ident[0:4, 0:4]), reads=[rrows, rconst], writes=[rPB])
            k.op("dve", lambda e: e.tensor_copy(out=modT[:].rearrange("p a b -> p (a b)"), in_=PB[:, 0:64]),
                 reads=[rPB], writes=[rmod])
            k.op("dve", lambda e: e.tensor_scalar(out=A1T[:], in0=modT[:, 8:16, :], scalar1=1.0, scalar2=None,
                                                  op0=ALU.add), reads=[rmod], writes=[rmod])
            k.op("dve", lambda e: e.tensor_tensor(out=A1T[:], in0=A1T[:],
                                                  in1=n1gT[:].unsqueeze(2).to_broadcast([128, 8, 4]),
                                                  op=ALU.mult), reads=[rmod, rconst], writes=[rmod])
            k.barrier()

        def bcast(b, col0):
            k.dma("sp", lambda e: e.dma_start(out=tmp4[0:4, :], in_=rowsd[:, col0:col0 + D]), rtmp4,
                  reads=[rrowsd], writes=[rtmp4])
            for half in range(2):
                k.op("pe", lambda e, half=half: e.matmul(
                    PC[:, half * 512:(half + 1) * 512], lhsT=sel[0:4, b, :],
                    rhs=tmp4[0:4, half * 512:(half + 1) * 512], start=True, stop=True),
                    reads=[rtmp4, rconst], writes=[rPC])

        def load_cast(dst, kbn, src_v, ncols, stg, rstg, rdst, cnt):
            for c0 in range(0, ncols, 256):
                i = cnt[0] % len(stg)
                cnt[0] += 1
                ld(stg[i][:, 0:kbn, :], src_v[:, :, c0:c0 + 256], rstg[i])
                if (c0 // 256) % 2 == 0:
                    k.op("dve", lambda e, i=i, c0=c0, kbn=kbn: e.tensor_copy(
                        out=dst[:, :, c0:c0 + 256], in_=stg[i][:, 0:kbn, :]), reads=[rstg[i]], writes=[rdst])
                else:
                    k.op("act", lambda e, i=i, c0=c0, kbn=kbn: e.copy(
                        out=dst[:, :, c0:c0 + 256], in_=stg[i][:, 0:kbn, :]), reads=[rstg[i]], writes=[rdst])

        def load_rows(dst, dcol, nkb, src_v, c0, C, stg, rstg, rdst, cnt):
            g = max(1, 2048 // C)
            for kb0 in range(0, nkb, g):
                gg_ = min(g, nkb - kb0)
                i = cnt[0] % len(stg)
                cnt[0] += 1
                sv = stg[i][:].rearrange("p a b -> p (a b)")[:, 0:gg_ * C].rearrange("p (a b) -> p a b", b=C)
                ld(sv, src_v[:, kb0:kb0 + gg_, c0:c0 + C], rstg[i])
                eng = "dve" if (cnt[0] % 2 == 0) else "act"
                if eng == "dve":
                    k.op("dve", lambda e, sv=sv, kb0=kb0, gg_=gg_: e.tensor_copy(
                        out=dst[:, kb0:kb0 + gg_, dcol:dcol + C], in_=sv), reads=[rstg[i]], writes=[rdst])
                else:
                    k.op("act", lambda e, sv=sv, kb0=kb0, gg_=gg_: e.copy(
                        out=dst[:, kb0:kb0 + gg_, dcol:dcol + C], in_=sv), reads=[rstg[i]], writes=[rdst])

        def norm_front(xin, rxin, W, pz):
            ssq, rssq, xs, rxs = W.ssqs[pz], W.rssqs[pz], W.xss[pz], W.rxss[pz]
            k.op("act", lambda e: e.activation(out=W.junk[:], in_=xin[:], func=AF.Square, accum_out=ssq[:, 0:1]),
                 reads=[rxin], writes=[W.rjunk, rssq])
            k.op("act", lambda e: e.activation(out=ssq[:, 1:2], in_=ssq[:, 0:1], func=AF.Sqrt,
                                               scale=1.0 / D, bias=epsb[:, 0:1]),
                 reads=[rssq, rconst], writes=[rssq])
            k.op("dve", lambda e: e.reciprocal(out=ssq[:, 2:3], in_=ssq[:, 1:2]), reads=[rssq], writes=[rssq])
            k.op("dve", lambda e: e.tensor_scalar(out=xs[:], in0=xin[:], scalar1=ssq[:, 2:3], scalar2=None,
                                                  op0=ALU.mult), reads=[rxin, rssq], writes=[rxs])

        def norm_back(j, W, pz):
            xs, rxs, hT, rhT = W.xss[pz], W.rxss[pz], W.hTs[pz], W.rhTs[pz]
            for blk in range(8):
                k.op("pe", lambda e, blk=blk: e.transpose(out=v8(PA)[:, blk, :], in_=xs[:, blk * 128:(blk + 1) * 128],
                                                          identity=ident[:]), reads=[rxs, rconst], writes=[rPA])
            for blk in range(8):
                k.op("act", lambda e, blk=blk: e.activation(
                    out=hT[:, blk, :], in_=v8(PA)[:, blk, :], func=AF.Identity,
                    scale=A1T[:, blk, j:j + 1], bias=modT[:, blk, j:j + 1]),
                    reads=[rPA, rmod], writes=[rhT])

        class WS:
            def sel(self, pz):
                self.ssq, self.rssq = self.ssqs[pz], self.rssqs[pz]
                self.xs, self.rxs = self.xss[pz], self.rxss[pz]
                self.hT, self.rhT = self.hTs[pz], self.rhTs[pz]

        def mk_work(stack, nstg=2):
            W = WS()
            W.xin = [sb(stack, [128, D], F32, "xin") for _ in range(2)]
            W.rxin = [Res("xin0"), Res("xin1")]
            W.junk = sb(stack, [128, D], BF16, "junk")
            W.rjunk = Res("junk")
            W.ssqs = [sb(stack, [128, 4], F32, "ssq") for _ in range(2)]
            W.rssqs = [Res("ssq0"), Res("ssq1")]
            W.xss = [sb(stack, [128, D], F32, "xs") for _ in range(2)]
            W.rxss = [Res("xs0"), Res("xs1")]
            W.hTs = [sb(stack, [128, 8, 128], BF16, "hT") for _ in range(2)]
            W.rhTs = [Res("hT0"), Res("hT1")]
            W.sel(0)
            W.stg = [sb(stack, [128, 8, 256], F32, "stg") for _ in range(nstg)]
            W.rstg = [Res("stg%d" % i) for i in range(nstg)]
            W.cnt = [0]
            return W

        uvb_d = nc.dram_tensor("uvb", [16384, 2 * D], BF16).ap()
        with ExitStack() as smix:
            YT = sb(smix, [128, 16, 8, 128], BF16, "YT")
            rYT = Res("YT")
            G1 = sb(smix, [128, D], F32, "G1")
            rG1 = Res("G1")
            NST = 4
            cbf = [sb(smix, [128, 2 * D], BF16, "cbf") for _ in range(NST)]
            rcbf = [Res("cbf%d" % i, bg=True) for i in range(NST)]
            rcst = [Res("cbs%d" % i, bg=True) for i in range(NST)]
            uv_v = uv_d.rearrange("(p r) c -> p r c", r=128)
            uvb_v = uvb_d.rearrange("(p r) c -> p r c", r=128)
            for t in range(128):
                i = t % NST
                k.dma("pool", lambda e, i=i, t=t: e.dma_start(out=cbf[i][:], in_=uv_v[:, t, :]), rcbf[i], writes=[rcbf[i]])
                k.dma("pool", lambda e, i=i, t=t: e.dma_start(out=uvb_v[:, t, :], in_=cbf[i][:]), rcst[i], reads=[rcbf[i]])
            for b in range(2):
                with ExitStack() as skv:
                    KT = sb(skv, [128, 4, 2304], BF16, "KT")
                    Vaug = sb(skv, [128, 18, 8, 65], BF16, "Vaug")
                    rKT, rV = Res("KT"), Res("V")
                    k.op("dve", lambda e: e.memset(Vaug[:], 1.0), writes=[rV])
                    with ExitStack() as sA:
                        W = mk_work(sA, 4)
                        Wkv = sb(sA, [128, 8, 1024], BF16, "Wkv")
                        rW = Res("Wkv")
                        load_rows(Wkv, 0, 8, win_v, 512, 1024, W.stg, W.rstg, rW, W.cnt)
                        def frontA(t):
                            sl = t % 2
                            src = x_d[b * SEQ + t * 128:b * SEQ + (t + 1) * 128, :] if t < 16 else \
                                ctx_d[b * 256 + (t - 16) * 128:b * 256 + (t - 15) * 128, :]
                            ld(W.xin[sl][:], src, W.rxin[sl])
                            norm_front(W.xin[sl], W.rxin[sl], W, sl)

                        frontA(0)
                        norm_back(b, W, 0)
                        for t in range(18):
                            sl = t % 2
                            W.sel(sl)
                            if t + 1 < 18:
                                frontA(t + 1)
                            for hp in range(4):
                                for kb in range(8):
                                    k.op("pe", lambda e, hp=hp, kb=kb: e.matmul(
                                        v8(PB)[:, hp, :], lhsT=Wkv[:, kb, hp * 128:(hp + 1) * 128], rhs=W.hT[:, kb, :],
                                        start=(kb == 0), stop=(kb == 7)), reads=[rW, W.rhT], writes=[rPB])
                            k.op("act", lambda e, t=t: e.copy(out=KT[:, :, t * 128:(t + 1) * 128], in_=v8(PB)[:, 0:4, :]),
                                 reads=[rPB], writes=[rKT])
                            for kb in range(8):
                                k.op("pe", lambda e, kb=kb: e.matmul(
                                    PC[:, 0:512], lhsT=W.hT[:, kb, :], rhs=Wkv[:, kb, 512:1024],
                                    start=(kb == 0), stop=(kb == 7)), reads=[rW, W.rhT], writes=[rPC])
                            k.op("dve", lambda e, t=t: e.tensor_copy(
                                out=Vaug[:, t, :, 0:64], in_=PC[:, 0:512].rearrange("p (h d) -> p h d", d=64)),
                                reads=[rPC], writes=[rV])
                            if t + 1 < 18:
                                norm_back(b if t + 1 < 16 else 2, W, (t + 1) % 2)
                        k.barrier()
                    with ExitStack() as sB:
                        W = mk_work(sB)
                        W3 = sb(sB, [128, 8, 1536], BF16, "W3")
                        rW3 = Res("W3")
                        biasb = sb(sB, [128, 8 * NTT, 128], BF16, "biasb")
                        rbias = Res("bias")
                        lng = sb(sB, [128, 512], F32, "lng")
                        wsTb = sb(sB, [128, 8, 128], BF16, "wsTb")
                        bsT = sb(sB, [128, 8], F32, "bsT")
                        rsm = Res("small")
                        qpad = sb(sB, [128, 8, 128], BF16, "qpad")
                        rq = Res("qpad")
                        u = sb(sB, [128, 512], F32, "u")
                        ru = Res("u")
                        gl = sb(sB, [128, 512], F32, "gl")
                        rgl = Res("gl")
                        st6 = sb(sB, [128, 8], F32, "st6")
                        rst = Res("st6")
                        vn = sb(sB, [128, 512], BF16, "vn")
                        rvn = Res("vn")
                        tt = sb(sB, [128, 512], F32, "tt")
                        rtt = Res("tt")
                        yab = sb(sB, [128, 1024], F32, "yab")
                        rya, ryb = Res("ya"), Res("yb")
                        PT = [sb(sB, [128, 8, 128], BF16, "PT") for _ in range(2)]
                        rPT = [Res("PT0"), Res("PT1")]
                        rcp = sb(sB, [128, 8], F32, "rcp")
                        rrcp = Res("rcp")
                        load_rows(W3, 0, 8, win_v, 0, 512, W.stg, W.rstg, rW3, W.cnt)
                        load_rows(W3, 512, 8, win_v, 1536, 1024, W.stg, W.rstg, rW3, W.cnt)
                        for c0 in range(0, 8 * NTT, 16):
                            n = min(16, 8 * NTT - c0)
                            i = W.cnt[0] % len(W.stg)
                            W.cnt[0] += 1
                            sv = W.stg[i][:].rearrange("p a b -> p (a b)")[:, 0:n * 128].rearrange("p (a b) -> p a b", b=128)
                            ld(sv, bias_d[:, c0:c0 + n, :], W.rstg[i])
                            k.op("dve", lambda e, sv=sv, c0=c0, n=n: e.tensor_copy(out=biasb[:, c0:c0 + n, :], in_=sv),
                                 reads=[W.rstg[i]], writes=[rbias])
                        i = W.cnt[0] % len(W.stg)
                        W.cnt[0] += 1
                        sv = W.stg[i][:].rearrange("p a b -> p (a b)")[:, 0:1024].rearrange("p (a b) -> p a b", b=128)
                        ld(sv, wsT_d[:, :, :], W.rstg[i])
                        k.op("dve", lambda e, sv=sv: e.tensor_copy(out=wsTb[:], in_=sv), reads=[W.rstg[i]], writes=[rsm])
                        ld(lng[:], lng_d[:, :], rsm)
                        ld(bsT[:], bsT_d[:, :], rsm)
                        k.op("dve", lambda e: e.memset(qpad[:], 0.0), writes=[rq])
                        if b == 0 or True:
                            bcast(b, 2048)
                            k.op("act", lambda e: e.copy(out=G1[:], in_=PC[:, :]), reads=[rPC], writes=[rG1])
                        def frontB(cc):
                            sl = cc % 2
                            ld(W.xin[sl][:], x_d[b * SEQ + cc * 128:b * SEQ + (cc + 1) * 128, :], W.rxin[sl])
                            norm_front(W.xin[sl], W.rxin[sl], W, sl)

                        frontB(0)
                        norm_back(b, W, 0)
                        for cch in range(16):
                            sl = cch % 2
                            W.sel(sl)
                            if cch + 1 < 16:
                                frontB(cch + 1)
                            for hp in range(4):
                                for kb in range(8):
                                    k.op("pe", lambda e, hp=hp, kb=kb: e.matmul(
                                        v8(PB)[:, hp, :], lhsT=W3[:, kb, hp * 128:(hp + 1) * 128], rhs=W.hT[:, kb, :],
                                        start=(kb == 0), stop=(kb == 7)), reads=[rW3, W.rhT], writes=[rPB])
                            qv = qpad[:].rearrange("p (hp two) t -> p hp two t", two=2)
                            k.op("act", lambda e: e.mul(out=qv[0:64, :, 0, :], in_=v8(PB)[0:64, 0:4, :], mul=0.125),
                                 reads=[rPB], writes=[rq])
                            k.op("act", lambda e: e.mul(out=qv[64:128, :, 1, :], in_=v8(PB)[64:128, 0:4, :], mul=0.125),
                                 reads=[rPB], writes=[rq])
                            for kb in range(8):
                                k.op("pe", lambda e, kb=kb: e.matmul(
                                    PC[:, 0:512], lhsT=W.hT[:, kb, :], rhs=W3[:, kb, 512:1024],
                                    start=(kb == 0), stop=(kb == 7)), reads=[rW3, W.rhT], writes=[rPC])
                            k.op("act", lambda e: e.activation(out=u[:], in_=PC[:, 0:512], func=AF.Gelu_apprx_tanh),
                                 reads=[rPC], writes=[ru])
                            for kb in range(8):
                                k.op("pe", lambda e, kb=kb: e.matmul(
                                    PD[:, 0:512], lhsT=W.hT[:, kb, :], rhs=W3[:, kb, 1024:1536],
                                    start=(kb == 0), stop=(kb == 7)), reads=[rW3, W.rhT], writes=[rPD])
                            k.op("act", lambda e: e.activation(out=gl[:], in_=PD[:, 0:512], func=AF.Gelu_apprx_tanh),
                                 reads=[rPD], writes=[rgl])
                            k.op("dve", lambda e: e.bn_stats(out=st6[:, 0:6], in_=gl[:]), reads=[rgl], writes=[rst])
                            k.op("dve", lambda e: e.bn_aggr(out=st6[:, 6:8], in_=st6[:, 0:6]), reads=[rst], writes=[rst])
                            k.op("act", lambda e: e.activation(out=st6[:, 0:1], in_=st6[:, 7:8], func=AF.Sqrt,
                                                               scale=1.0, bias=epsb[:, 0:1]),
                                 reads=[rst, rconst], writes=[rst])
                            k.op("dve", lambda e: e.reciprocal(out=st6[:, 1:2], in_=st6[:, 0:1]), reads=[rst], writes=[rst])
                            k.op("dve", lambda e: e.tensor_scalar(out=vn[:], in0=gl[:], scalar1=st6[:, 6:7],
                                                                  scalar2=st6[:, 1:2], op0=ALU.subtract, op1=ALU.mult),
                                 reads=[rgl, rst], writes=[rvn])
                            if cch + 1 < 16:
                                norm_back(b, W, (cch + 1) % 2)
                            for g in range(8):
                                k.op("pe", lambda e, g=g: e.matmul(
                                    PD[:, 512 + g * 64:512 + (g + 1) * 64], lhsT=wsTb[:, g, :],
                                    rhs=vn[:, g * 64:(g + 1) * 64], start=True, stop=True),
                                    reads=[rsm, rvn], writes=[rPD])
                            k.op("dve", lambda e: e.tensor_tensor(out=tt[:], in0=PD[:, 512:1024], in1=lng[:], op=ALU.mult),
                                 reads=[rPD, rsm], writes=[rtt])
                            ttv = tt[:].rearrange("p (g d) -> p g d", d=64)
                            k.op("dve", lambda e: e.tensor_tensor(out=ttv, in0=ttv,
                                                                  in1=bsT[:].unsqueeze(2).to_broadcast([128, 8, 64]),
                                                                  op=ALU.add), reads=[rtt, rsm], writes=[rtt])
                            k.op("dve", lambda e: e.tensor_tensor(out=yab[:, 512:1024], in0=tt[:], in1=u[:], op=ALU.mult),
                                 reads=[rtt, ru], writes=[ryb])
                            klist = CHUNK_KEYS[cch]
                            nb = len(klist)
                            nj = nb + 2
                            def att_S(h):
                                hp = h // 2
                                S, rS = (PC, rPC) if h % 2 == 0 else (PD, rPD)
                                Sv = v8(S)
                                for j, (kc, ti) in enumerate(klist):
                                    k.op("pe", lambda e, j=j, kc=kc, Sv=Sv, hp=hp, h=h: e.matmul(
                                        Sv[:, j, :], lhsT=KT[:, hp, kc * 128:(kc + 1) * 128], rhs=qpad[:, h, :],
                                        start=True, stop=False), reads=[rKT, rq], writes=[rS])
                                    k.op("pe", lambda e, j=j, ti=ti, Sv=Sv, h=h: e.matmul(
                                        Sv[:, j, :], lhsT=biasb[:, h * NTT + ti, :], rhs=identb[:],
                                        start=False, stop=True), reads=[rbias, rconst], writes=[rS])
                                for c in range(2):
                                    k.op("pe", lambda e, c=c, Sv=Sv, hp=hp, h=h: e.matmul(
                                        Sv[:, nb + c, :], lhsT=KT[:, hp, 2048 + c * 128:2048 + (c + 1) * 128],
                                        rhs=qpad[:, h, :], start=True, stop=True), reads=[rKT, rq], writes=[rS])
                                pt, rpt = PT[h % 2], rPT[h % 2]
                                k.op("act", lambda e, pt=pt, Sv=Sv: e.activation(out=pt[:, 0:nj, :], in_=Sv[:, 0:nj, :],
                                                                                 func=AF.Exp),
                                     reads=[rS], writes=[rpt])

                            def att_PV(h):
                                pt, rpt = PT[h % 2], rPT[h % 2]
                                ocol = (h // 4) * 512 + (h % 4) * 65
                                for j in range(nj):
                                    vt = klist[j][0] if j < nb else 16 + (j - nb)
                                    k.op("pe", lambda e, j=j, vt=vt, pt=pt, h=h, ocol=ocol: e.matmul(
                                        PB[:, ocol:ocol + 65], lhsT=pt[:, j, :], rhs=Vaug[:, vt, h, :],
                                        start=(j == 0), stop=(j == nj - 1)), reads=[rpt, rV], writes=[rPB])

                            for h in range(9):
                                if h < 8:
                                    att_S(h)
                                if h >= 1:
                                    att_PV(h - 1)
                            for a in range(2):
                                Ov = PB[:, a * 512:a * 512 + 260].rearrange("p (h e) -> p h e", e=65)
                                k.op("dve", lambda e, a=a, Ov=Ov: e.reciprocal(out=rcp[:, a * 4:(a + 1) * 4], in_=Ov[:, :, 64]),
                                     reads=[rPB], writes=[rrcp])
                                k.op("dve", lambda e, a=a, Ov=Ov: e.tensor_tensor(
                                    out=yab[:, a * 256:(a + 1) * 256].rearrange("p (h d) -> p h d", d=64),
                                    in0=Ov[:, :, 0:64],
                                    in1=rcp[:, a * 4:(a + 1) * 4].unsqueeze(2).to_broadcast([128, 4, 64]), op=ALU.mult),
                                    reads=[rPB, rrcp], writes=[rya])
                            for blk in range(8):
                                k.op("pe", lambda e, blk=blk: e.transpose(out=v8(PD)[:, blk, :],
                                                                          in_=yab[:, blk * 128:(blk + 1) * 128],
                                                                          identity=ident[:]),
                                     reads=[rya, ryb, rconst], writes=[rPD])
                            k.op("act", lambda e, cch=cch: e.copy(out=YT[:, cch, :, :], in_=v8(PD)[:, :, :]),
                                 reads=[rPD], writes=[rYT])
                        k.barrier()
                with ExitStack() as sC:
                    W = mk_work(sC, 4)
                    Wg = sb(sC, [128, 8, 2048], BF16, "Wg")
                    Wpa = sb(sC, [128, 4, 1024], BF16, "Wpa")
                    Wpb = sb(sC, [128, 4, 1024], BF16, "Wpb")
                    Wo = sb(sC, [128, 8, 1024], BF16, "Wo")
                    rWg, rWp, rWo = Res("Wg"), Res("Wp"), Res("Wo")
                    sga = sb(sC, [128, 8, 128], F32, "sga")
                    sgb = sb(sC, [128, 8, 128], F32, "sgb")
                    rsga, rsgb = Res("sga"), Res("sgb")
                    t1 = sb(sC, [128, 1024], F32, "t1")
                    t2 = sb(sC, [128, 1024], F32, "t2")
                    rt1, rt2 = Res("t1"), Res("t2")
                    mT = sb(sC, [128, 8, 128], BF16, "mT")
                    rmT = Res("mT")
                    x1 = [sb(sC, [128, 1024], F32, "x1") for _ in range(2)]
                    rx1 = [Res("x1a"), Res("x1b")]
                    load_rows(Wg, 0, 8, win_v, 2560, 2048, W.stg, W.rstg, rWg, W.cnt)
                    wout_v = wout_d.rearrange("(kb p) c -> p kb c", p=128)
                    load_rows(Wo, 0, 8, wout_v, 0, 1024, W.stg, W.rstg, rWo, W.cnt)
                    load_rows(Wpa, 0, 4, wpa_d.rearrange("(kb p) c -> p kb c", p=128), 0, 1024, W.stg, W.rstg, rWp, W.cnt)
                    load_rows(Wpb, 0, 4, wpb_d.rearrange("(kb p) c -> p kb c", p=128), 0, 1024, W.stg, W.rstg, rWp, W.cnt)
                    def frontC(cc):
                        sl = cc % 2
                        ld(W.xin[sl][:], x_d[b * SEQ + cc * 128:b * SEQ + (cc + 1) * 128, :], W.rxin[sl])
                        norm_front(W.xin[sl], W.rxin[sl], W, sl)

                    frontC(0)
                    norm_back(b, W, 0)
                    for cch in range(16):
                        sl = cch % 2
                        r0 = b * SEQ + cch * 128
                        W.sel(sl)
                        if cch + 1 < 16:
                            frontC(cch + 1)
                        for (c_off, P_, rP_, sg, rsg) in ((0, PB, rPB, sga, rsga), (1024, PC, rPC, sgb, rsgb)):
                            for ob in range(8):
                                for kb in range(8):
                                    k.op("pe", lambda e, ob=ob, kb=kb, P_=P_, c_off=c_off: e.matmul(
                                        v8(P_)[:, ob, :], lhsT=Wg[:, kb, c_off + ob * 128:c_off + (ob + 1) * 128],
                                        rhs=W.hT[:, kb, :], start=(kb == 0), stop=(kb == 7)),
                                        reads=[rWg, W.rhT], writes=[rP_])
                            k.op("act", lambda e, P_=P_, sg=sg: e.activation(out=sg[:], in_=v8(P_)[:, :, :], func=AF.Sigmoid),
                                 reads=[rP_], writes=[rsg])
                        if cch + 1 < 16:
                            norm_back(b, W, (cch + 1) % 2)
                        for (y0, P_, rP_, wt) in ((0, PD, rPD, Wpa), (4, PB, rPB, Wpb)):
                            for ob in range(8):
                                for kb in range(4):
                                    k.op("pe", lambda e, ob=ob, kb=kb, P_=P_, wt=wt, y0=y0: e.matmul(
                                        v8(P_)[:, ob, :], lhsT=wt[:, kb, ob * 128:(ob + 1) * 128],
                                        rhs=YT[:, cch, y0 + kb, :], start=(kb == 0), stop=(kb == 3)),
                                        reads=[rWp, rYT], writes=[rP_])
                        k.op("dve", lambda e: e.tensor_tensor(out=t1[:], in0=PD[:, :], in1=sga[:].rearrange("p a b -> p (a b)"),
                                                              op=ALU.mult), reads=[rPD, rsga], writes=[rt1])
                        k.op("dve", lambda e: e.tensor_tensor(out=t2[:], in0=PB[:, :], in1=sgb[:].rearrange("p a b -> p (a b)"),
                                                              op=ALU.mult), reads=[rPB, rsgb], writes=[rt2])
                        k.op("dve", lambda e: e.tensor_tensor(out=mT[:].rearrange("p a b -> p (a b)"), in0=t1[:], in1=t2[:],
                                                               op=ALU.add), reads=[rt1, rt2], writes=[rmT])
                        for half in range(2):
                            for kb in range(8):
                                k.op("pe", lambda e, half=half, kb=kb: e.matmul(
                                    PC[:, half * 512:(half + 1) * 512], lhsT=mT[:, kb, :],
                                    rhs=Wo[:, kb, half * 512:(half + 1) * 512], start=(kb == 0), stop=(kb == 7)),
                                    reads=[rmT, rWo], writes=[rPC])
                        k.op("dve", lambda e: e.tensor_tensor(out=t1[:], in0=PC[:, :], in1=G1[:], op=ALU.mult),
                             reads=[rPC, rG1], writes=[rt1])
                        k.op("dve", lambda e, sl=sl: e.tensor_tensor(out=x1[sl][:], in0=t1[:], in1=W.xin[sl][:], op=ALU.add),
                             reads=[rt1, W.rxin[sl]], writes=[rx1[sl]])
                        k.dma("sp", lambda e, sl=sl, r0=r0: e.dma_start(out=out_d[r0:r0 + 128, :], in_=x1[sl][:]),
                              rx1[sl], reads=[rx1[sl]])
                    k.barrier()
            k.barrier(full=True)

        GD = BF16
        NBUF = 12
        GRP = 4
        PF = NBUF // GRP - 1
        G_INS = 2
        with ExitStack() as sP:
            wq = sb(sP, [128, 8, 2048], BF16, "wq")
            keysT = sb(sP, [128, 16, 128], BF16, "keysT")
            rwq, rkeys = Res("wq"), Res("keys")
            fg = sb(sP, [128, D], F32, "fg")
            iota = sb(sP, [128, 16], F32, "iota")
            rpc = Res("pconst")
            _A2 = sb(sP, [128, D], F32, "A2")
            _B2 = sb(sP, [128, D], F32, "B2")
            A2 = [_A2, _A2]
            B2 = [_B2, _B2]
            G2 = [sb(sP, [128, D], F32, "G2") for _ in range(2)]
            rmod2 = Res("mod2")
            cnt = [0]
            with ExitStack() as sPs:
                wq_v = wq_d.rearrange("(kb p) c -> p kb c", p=128)
                stg = [sb(sPs, [128, 8, 256], F32, "pstg") for _ in range(4)]
                rstg = [Res("pstg%d" % i) for i in range(4)]
                n2g = sb(sPs, [128, D], F32, "n2g")
                load_rows(wq, 0, 8, wq_v, 0, 2048, stg, rstg, rwq, cnt)
                i = cnt[0] % 4
                cnt[0] += 1
                sv = stg[i][:].rearrange("p a b -> p (a b)").rearrange("p (a b) -> p a b", b=128)
                ld(sv, keysT_d[:, :, :], rstg[i])
                k.op("dve", lambda e, sv=sv: e.tensor_copy(out=keysT[:], in_=sv), reads=[rstg[i]], writes=[rkeys])
                ld(n2g[:], n2g_d[:, :], rpc)
                ld(fg[:], fg_d[:, :], rpc)
                ld(iota[:], iota_d[:, :], rpc)
                rg2 = Res("g2")
                bcast(0, 4096)
                k.op("dve", lambda e: e.scalar_tensor_tensor(out=A2[0][:], in0=PC[:, :], scalar=1.0, in1=n2g[:],
                                                             op0=ALU.add, op1=ALU.mult),
                     reads=[rPC, rpc], writes=[rmod2])
                bcast(0, 3072)
                k.op("act", lambda e: e.copy(out=B2[0][:], in_=PC[:, :]), reads=[rPC], writes=[rmod2])
                for b in range(2):
                    bcast(b, 5120)
                    k.op("act", lambda e, b=b: e.copy(out=G2[b][:], in_=PC[:, :]), reads=[rPC], writes=[rg2])
                k.barrier()
            x1t = [sb(sP, [128, D], F32, "x1t") for _ in range(2)]
            rx1t = [Res("x1t0"), Res("x1t1")]
            h2 = [sb(sP, [128, D], F32, "h2") for _ in range(2)]
            rh2 = [Res("h20"), Res("h21")]
            junkb = sb(sP, [128, D], BF16, "pjunkb")
            rjunkb = Res("pjunkb")
            junkd = sb(sP, [128, D], BF16, "pjunkd")
            rjunkd = Res("pjunkd")
            ssq = sb(sP, [128, 8], F32, "pssq")
            rssq = Res("pssq")
            ssq3 = sb(sP, [128, 8], F32, "pssq3")
            rssq3 = Res("pssq3")
            h2T = sb(sP, [128, 8, 128], BF16, "h2T")
            rh2T = Res("h2T")
            qT = sb(sP, [128, 16, 128], BF16, "qT")
            rqT = Res("qT")
            W0 = sb(sP, [128, 2048], F32, "W0")
            W1 = sb(sP, [128, 2048], F32, "W1")
            rW0, rW1 = Res("W0"), Res("W1")
            top1 = sb(sP, [128, 16, 16], F32, "top1")
            idx1 = sb(sP, [128, 16, 16], U32, "idx1")
            idx1f = sb(sP, [128, 16, 16], F32, "idx1f")
            rtop1, ridx1 = Res("top1"), Res("idx1")
            best = sb(sP, [128, 8, 16], F32, "best")
            pos = sb(sP, [128, 8, 16], U32, "pos")
            rbest, rpos = Res("best"), Res("pos")
            pa_u = sb(sP, [128, 8, 16], U32, "pa_u")
            pb_u = sb(sP, [128, 8, 16], U32, "pb_u")
            pa_f = sb(sP, [128, 8, 16], F32, "pa_f")
            pb_f = sb(sP, [128, 8, 16], F32, "pb_f")
            i0f = sb(sP, [128, 8, 16], F32, "i0f")
            i1f = sb(sP, [128, 8, 16], F32, "i1f")
            rdec = Res("dec")
            ef = sb(sP, [128, 128], F32, "ef")
            ei = [sb(sP, [128, 128], I32, "ei") for _ in range(2)]
            rei = [Res("ei0"), Res("ei1")]
            gate = [sb(sP, [128, 128], F32, "gate") for _ in range(2)]
            rgate = [Res("gate0"), Res("gate1")]
            gtmp = sb(sP, [128, 8, 16], F32, "gtmp")
            gsum = sb(sP, [128, 16], F32, "gsum")
            rgt = Res("gtmp")
            NGR = 128 // GRP
            actv = sb(sP, [128, 128], F32, "actv")
            gel = sb(sP, [128, 128], F32, "gel")
            wgt = sb(sP, [128, 128], F32, "wgt")
            ract = [Res("act%d" % g) for g in range(NGR)]
            rgel = [Res("gel%d" % g) for g in range(NGR)]
            rwgt = [Res("wgt%d" % g) for g in range(NGR)]
            Gb = [sb(sP, [128, 2 * D], GD, "Gb") for _ in range(NBUF)]
            rGb = [Res("Gb%d" % i) for i in range(NBUF)]
            NDG = 8
            dg = [sb(sP, [128, 128], GD, "dg") for _ in range(NDG)]
            rdg = [Res("dg%d" % i) for i in range(NDG)]
            identg = ident if GD == F32 else identb
            x2 = W0
            rx2 = rW0
            t3 = W1
            rt3 = rW1

            def topk16(src3, rsrc, dst_top, dst_idx, rtop, ridx, scratch3, rscr, ngrp):
                for g in range(ngrp):
                    k.op("dve", lambda e, g=g: e.max(out=dst_top[:, g, 0:8], in_=src3[:, g, :]), reads=[rsrc], writes=[rtop])
                    if g % 8 == 7:
                        yield
                for g in range(ngrp):
                    k.op("dve", lambda e, g=g: e.max_index(out=dst_idx[:, g, 0:8], in_max=dst_top[:, g, 0:8],
                                                          in_values=src3[:, g, :]), reads=[rsrc, rtop], writes=[ridx])
                    if g % 8 == 7:
                        yield
                for g in range(ngrp):
                    k.op("dve", lambda e, g=g: e.match_replace(out=scratch3[:, g, :], in_to_replace=dst_top[:, g, 0:8],
                                                              in_values=src3[:, g, :], imm_value=-1e30),
                         reads=[rsrc, rtop], writes=[rscr])
                    if g % 8 == 7:
                        yield
                for g in range(ngrp):
                    k.op("dve", lambda e, g=g: e.max(out=dst_top[:, g, 8:16], in_=scratch3[:, g, :]),
                         reads=[rscr], writes=[rtop])
                    if g % 8 == 7:
                        yield
                for g in range(ngrp):
                    k.op("dve", lambda e, g=g: e.max_index(out=dst_idx[:, g, 8:16], in_max=dst_top[:, g, 8:16],
                                                          in_values=scratch3[:, g, :]), reads=[rscr, rtop], writes=[ridx])
                    if g % 8 == 7:
                        yield

            def stage1(ci):
                b = ci // 16
                sl = ci % 2
                r0 = ci * 128
                if ci == 16:
                    ld(W1[:, 0:D], n2g_d[:, :], rW1)
                    bcast(1, 4096)
                    k.op("dve", lambda e: e.scalar_tensor_tensor(out=A2[1][:], in0=PC[:, :], scalar=1.0, in1=W1[:, 0:D],
                                                                 op0=ALU.add, op1=ALU.mult),
                         reads=[rPC, rW1], writes=[rmod2])
                    bcast(1, 3072)
                    k.op("act", lambda e: e.copy(out=B2[1][:], in_=PC[:, :]), reads=[rPC], writes=[rmod2])
                ld(x1t[sl][:], out_d[r0:r0 + 128, :], rx1t[sl])
                k.op("act", lambda e: e.activation(out=junkb[:], in_=x1t[sl][:], func=AF.Square, accum_out=ssq[:, 0:1]),
                     reads=[rx1t[sl]], writes=[rjunkb, rssq])
                k.op("act", lambda e: e.activation(out=ssq[:, 1:2], in_=ssq[:, 0:1], func=AF.Sqrt, scale=1.0 / D,
                                                   bias=epsb[:, 0:1]), reads=[rssq, rconst], writes=[rssq])
                k.op("dve", lambda e: e.reciprocal(out=ssq[:, 2:3], in_=ssq[:, 1:2]), reads=[rssq], writes=[rssq])
                k.op("dve", lambda e: e.scalar_tensor_tensor(out=h2[sl][:], in0=x1t[sl][:], scalar=ssq[:, 2:3],
                                                             in1=A2[b][:], op0=ALU.mult, op1=ALU.mult),
                     reads=[rx1t[sl], rssq, rmod2], writes=[rh2[sl]])
                k.op("dve", lambda e: e.tensor_tensor(out=h2[sl][:], in0=h2[sl][:], in1=B2[b][:], op=ALU.add),
                     reads=[rh2[sl], rmod2], writes=[rh2[sl]])
                for blk in range(8):
                    k.op("pe", lambda e, blk=blk: e.transpose(out=v8(PA)[:, blk, :],
                                                              in_=h2[sl][:, blk * 128:(blk + 1) * 128],
                                                              identity=ident[:]),
                         reads=[rh2[sl], rconst], writes=[rPA])
                k.op("act", lambda e: e.copy(out=h2T[:], in_=v8(PA)[:, :, :]), reads=[rPA], writes=[rh2T])
                yield
                for half, (P_, rP_) in enumerate(((PB, rPB), (PC, rPC))):
                    for ob in range(8):
                        gi = half * 8 + ob
                        for kb in range(8):
                            k.op("pe", lambda e, ob=ob, gi=gi, kb=kb, P_=P_: e.matmul(
                                v8(P_)[:, ob, :], lhsT=wq[:, kb, gi * 128:(gi + 1) * 128], rhs=h2T[:, kb, :],
                                start=(kb == 0), stop=(kb == 7)), reads=[rwq, rh2T], writes=[rP_])
                    k.op("act", lambda e, half=half, P_=P_: e.copy(out=qT[:, half * 8:(half + 1) * 8, :], in_=v8(P_)[:, :, :]),
                         reads=[rP_], writes=[rqT])
                    yield
                s3 = W0[:].rearrange("p (g k) -> p g k", k=128)
                s3b = W1[:].rearrange("p (g k) -> p g k", k=128)
                for half, (P_, rP_) in enumerate(((PA, rPA), (PB, rPB))):
                    for ob in range(8):
                        gi = half * 8 + ob
                        k.op("pe", lambda e, ob=ob, gi=gi, P_=P_: e.matmul(
                            v8(P_)[:, ob, :], lhsT=qT[:, gi, :], rhs=keysT[:, gi, :], start=True, stop=True),
                            reads=[rqT, rkeys], writes=[rP_])
                    k.op("act", lambda e, half=half, P_=P_: e.copy(out=s3[:, half * 8:(half + 1) * 8, :], in_=v8(P_)[:, :, :]),
                         reads=[rP_], writes=[rW0])
                yield
                yield from topk16(s3, rW0, top1, idx1, rtop1, ridx1, s3b, rW1, 16)
                t1v = top1[:].rearrange("p (h two) k -> p h two k", two=2)
                cand = W0[:].rearrange("p (h a b) -> p h a b", a=16, b=16)
                k.op("dve", lambda e: e.tensor_tensor(
                    out=cand, in0=t1v[:, :, 0, :].unsqueeze(3).to_broadcast([128, 8, 16, 16]),
                    in1=t1v[:, :, 1, :].unsqueeze(2).to_broadcast([128, 8, 16, 16]), op=ALU.add),
                    reads=[rtop1], writes=[rW0])
                c3 = W0[:].rearrange("p (h c) -> p h c", c=256)
                c3b = W1[:].rearrange("p (h c) -> p h c", c=256)
                yield
                yield from topk16(c3, rW0, best, pos, rbest, rpos, c3b, rW1, 8)
                k.op("dve", lambda e: e.tensor_single_scalar(out=pa_u[:], in_=pos[:], scalar=4, op=ALU.arith_shift_right),
                     reads=[rpos], writes=[rdec])
                k.op("dve", lambda e: e.tensor_single_scalar(out=pb_u[:], in_=pos[:], scalar=15, op=ALU.bitwise_and),
                     reads=[rpos], writes=[rdec])
                k.op("dve", lambda e: e.tensor_copy(out=pa_f[:], in_=pa_u[:]), reads=[rdec], writes=[rdec])
                k.op("dve", lambda e: e.tensor_copy(out=pb_f[:], in_=pb_u[:]), reads=[rdec], writes=[rdec])
                k.op("dve", lambda e: e.tensor_copy(out=idx1f[:], in_=idx1[:]), reads=[ridx1], writes=[ridx1])
                yield
                i1v = idx1f[:].rearrange("p (h two) k -> p h two k", two=2)
                oh = W1[:].rearrange("p (h a b) -> p h a b", a=16, b=16)
                for (pf, half, dst) in ((pa_f, 0, i0f), (pb_f, 1, i1f)):
                    k.op("dve", lambda e, pf=pf: e.tensor_tensor(
                        out=oh, in0=pf[:].unsqueeze(3).to_broadcast([128, 8, 16, 16]),
                        in1=iota[:].unsqueeze(1).unsqueeze(1).to_broadcast([128, 8, 16, 16]), op=ALU.is_equal),
                        reads=[rdec, rpc], writes=[rW1])
                    k.op("dve", lambda e, half=half: e.tensor_tensor(
                        out=oh, in0=oh, in1=i1v[:, :, half, :].unsqueeze(2).to_broadcast([128, 8, 16, 16]), op=ALU.mult),
                        reads=[rW1, ridx1], writes=[rW1])
                    k.op("dve", lambda e, dst=dst: e.tensor_reduce(out=dst[:], in_=oh, axis=AX.X, op=ALU.add),
                         reads=[rW1], writes=[rdec])
                    yield
                k.op("dve", lambda e: e.scalar_tensor_tensor(
                    out=ef[:], in0=i0f[:].rearrange("p h k -> p (h k)"), scalar=128.0,
                    in1=i1f[:].rearrange("p h k -> p (h k)"), op0=ALU.mult, op1=ALU.add), reads=[rdec], writes=[rdec])
                k.op("dve", lambda e: e.tensor_copy(out=ei[sl][:], in_=ef[:]), reads=[rdec], writes=[rei[sl]])
                k.op("dve", lambda e: e.tensor_tensor(out=gtmp[:], in0=best[:],
                                                      in1=best[:, :, 0:1].to_broadcast([128, 8, 16]), op=ALU.subtract),
                     reads=[rbest], writes=[rgt])
                k.op("act", lambda e: e.activation(out=gtmp[:], in_=gtmp[:], func=AF.Exp), reads=[rgt], writes=[rgt])
                k.op("dve", lambda e: e.tensor_reduce(out=gsum[:, 0:8], in_=gtmp[:], axis=AX.X, op=ALU.add),
                     reads=[rgt], writes=[rgt])
                k.op("dve", lambda e: e.reciprocal(out=gsum[:, 8:16], in_=gsum[:, 0:8]), reads=[rgt], writes=[rgt])
                k.op("dve", lambda e: e.tensor_tensor(out=gate[sl][:].rearrange("p (h k) -> p h k", k=16), in0=gtmp[:],
                                                      in1=gsum[:, 8:16].unsqueeze(2).to_broadcast([128, 8, 16]),
                                                      op=ALU.mult), reads=[rgt], writes=[rgate[sl]])

            def gathers(gg):
                ci, g = divmod(gg, NGR)
                sl = ci % 2
                for kk in range(GRP):
                    s = g * GRP + kk
                    bi = (gg * GRP + kk) % NBUF
                    k.dma("pool", lambda e, s=s, bi=bi: e.indirect_dma_start(
                        out=Gb[bi][:], out_offset=None, in_=uv_src[:, :],
                        in_offset=bass.IndirectOffsetOnAxis(ap=ei[sl][:, s:s + 1], axis=0)),
                        rGb[bi], reads=[rei[sl]], writes=[rGb[bi]])

            def compute_a(gg):
                ci, g = divmod(gg, NGR)
                sl = ci % 2
                c0, c1 = g * GRP, (g + 1) * GRP
                for kk in range(GRP):
                    s = g * GRP + kk
                    bi = (gg * GRP + kk) % NBUF
                    k.op("dve", lambda e, s=s, bi=bi: e.scalar_tensor_tensor(
                        out=junkd[:], in0=Gb[bi][:, 0:D], scalar=1.0, in1=h2[sl][:], op0=ALU.mult, op1=ALU.mult,
                        accum_out=actv[:, s:s + 1]), reads=[rGb[bi], rh2[sl]], writes=[rjunkd, ract[g]])
                k.op("act", lambda e: e.activation(out=gel[:, c0:c1], in_=actv[:, c0:c1], func=AF.Gelu_apprx_tanh),
                     reads=[ract[g]], writes=[rgel[g]])

            def compute_b(gg):
                ci, g = divmod(gg, NGR)
                sl = ci % 2
                c0, c1 = g * GRP, (g + 1) * GRP
                k.op("dve", lambda e: e.tensor_tensor(out=wgt[:, c0:c1], in0=gel[:, c0:c1], in1=gate[sl][:, c0:c1],
                                                      op=ALU.mult), reads=[rgel[g], rgate[sl]], writes=[rwgt[g]])
                for kk in range(GRP):
                    s = g * GRP + kk
                    bi = (gg * GRP + kk) % NBUF
                    di = (gg * GRP + kk) % NDG
                    k.op("act", lambda e, s=s, di=di: e.activation(out=dg[di][:], in_=identg[:], func=AF.Copy,
                                                                  scale=wgt[:, s:s + 1]),
                         reads=[rwgt[g], rconst], writes=[rdg[di]])
                    for half in range(2):
                        k.op("pe", lambda e, s=s, bi=bi, di=di, half=half: e.matmul(
                            PD[:, half * 512:(half + 1) * 512], lhsT=dg[di][:],
                            rhs=Gb[bi][:, D + half * 512:D + (half + 1) * 512],
                            start=(s == 0), stop=(s == 127)), reads=[rdg[di], rGb[bi]], writes=[rPD])

            def finalize(ci):
                b = ci // 16
                sl = ci % 2
                r0 = ci * 128
                k.op("dve", lambda e: e.tensor_tensor(out=t3[:, 0:D], in0=PD[:, :], in1=G2[b][:], op=ALU.mult),
                     reads=[rPD, rg2], writes=[rt3])
                k.op("dve", lambda e: e.tensor_tensor(out=x2[:, 0:D], in0=t3[:, 0:D], in1=x1t[sl][:], op=ALU.add),
                     reads=[rt3, rx1t[sl]], writes=[rx2])
                k.op("act", lambda e: e.activation(out=junkb[:], in_=x2[:, 0:D], func=AF.Square, accum_out=ssq3[:, 0:1]),
                     reads=[rx2], writes=[rjunkb, rssq3])
                k.op("act", lambda e: e.activation(out=ssq3[:, 1:2], in_=ssq3[:, 0:1], func=AF.Sqrt, scale=1.0 / D,
                                                   bias=epsb[:, 0:1]), reads=[rssq3, rconst], writes=[rssq3])
                k.op("dve", lambda e: e.reciprocal(out=ssq3[:, 2:3], in_=ssq3[:, 1:2]), reads=[rssq3], writes=[rssq3])
                k.op("dve", lambda e: e.scalar_tensor_tensor(out=x2[:, 0:D], in0=x2[:, 0:D], scalar=ssq3[:, 2:3], in1=fg[:],
                                                             op0=ALU.mult, op1=ALU.mult),
                     reads=[rx2, rssq3, rpc], writes=[rx2])
                k.dma("sp", lambda e: e.dma_start(out=out_d[r0:r0 + 128, :], in_=x2[:, 0:D]), rx2, reads=[rx2])

            uv_src = uvb_d
            NCH = 32
            TOT = NCH * NGR
            for _ in stage1(0):
                pass
            for gg in range(min(PF, TOT)):
                gathers(gg)
            gen = None
            for gg in range(TOT):
                ci, g = divmod(gg, NGR)
                if g == G_INS and ci + 1 < NCH:
                    gen = stage1(ci + 1)
                if gg + PF < TOT:
                    gathers(gg + PF)
                compute_a(gg)
                if gen is not None:
                    if g >= NGR - PF - 1:
                        for _ in gen:
                            pass
                        gen = None
                    elif next(gen, "done") == "done":
                        gen = None
                compute_b(gg)
                if g == NGR - 1:
                    finalize(ci)
            k.barrier()
    return nc


_NC_CACHE = {}


def kernel(x, c, ctx, c_ctx, ada_w, ada_b, norm1_g, norm2_g, w_in, na_rpb, gm_ln_g, gm_ws, gm_bs,
           w_proj_a, w_proj_b, w_out, peer_wq, peer_keys, peer_u, peer_v, final_g):
    f = lambda a: np.ascontiguousarray(np.asarray(a, dtype=np.float32))
    x, c, ctx, c_ctx = f(x), f(c), f(ctx), f(c_ctx)
    if "nc" not in _NC_CACHE:
        _NC_CACHE["nc"] = build_program()
    nc = _NC_CACHE["nc"]
    shared = {
        "ada_w": f(ada_w)[0],
        "adab_rep": f(np.broadcast_to(f(ada_b)[0][None, :], (4, 6 * D))),
        "n1gT": f(f(norm1_g)[0].reshape(8, 128).T),
        "n2g_rep": f(np.broadcast_to(f(norm2_g)[0][None, :], (128, D))),
        "fg_rep": f(np.broadcast_to(f(final_g)[None, :], (128, D))),
        "lng_rep": f(np.broadcast_to(f(gm_ln_g)[0][None, :], (128, 512))),
        "w_in": f(w_in)[0],
        "bias_t": _bias_tiles(f(na_rpb)[0]),
        "wsT": f(np.transpose(f(gm_ws)[0], (2, 0, 1))),
        "bsT": f(f(gm_bs)[0].T),
        "w_pa": f(w_proj_a)[0],
        "w_pb": f(w_proj_b)[0],
        "w_out": f(w_out)[0],
        "peer_wq": f(peer_wq)[0],
        "keysT": f(np.transpose(f(peer_keys)[0].reshape(16, 128, 128), (2, 0, 1))),
        "peer_uv": f(np.concatenate([f(peer_u)[0], f(peer_v)[0]], axis=1)),
        "ident": np.eye(128, dtype=np.float32),
        "iota16": f(np.broadcast_to(np.arange(16, dtype=np.float32)[None, :], (128, 16))),
        "sel": f(np.stack([np.stack([np.full(128, 1.0 if kk == bb else 0.0, np.float32) for bb in range(2)])
                           for kk in range(4)])),
    }
    in_maps = []
    for core in range(NCORES):
        b0 = 2 * core
        cv = np.stack([c[b0], c[b0 + 1], c_ctx, c_ctx], axis=1)
        m = dict(shared)
        m["x"] = f(x[b0:b0 + 2].reshape(2 * SEQ, D))
        m["ctx"] = f(ctx[b0:b0 + 2].reshape(512, D))
        m["cT"] = f(cv.reshape(8, 128, 4).transpose(1, 0, 2))
        in_maps.append(m)
    res = run_bass_kernel_spmd(nc, in_maps, core_ids=list(range(NCORES)))
    outs = [np.asarray(r["out"], dtype=np.float32).reshape(2, SEQ, D) for r in res.results]
    return np.concatenate(outs, axis=0)
```

```python
import numpy as np
from contextlib import ExitStack
import concourse.bass as bass
import concourse.mybir as mybir
from concourse.bass_utils import run_bass_kernel_spmd

F32 = mybir.dt.float32
BF16 = mybir.dt.bfloat16
I32 = mybir.dt.int32
U32 = mybir.dt.uint32
AF = mybir.ActivationFunctionType
ALU = mybir.AluOpType
AX = mybir.AxisListType
EPS = 1e-6
NEG = -30000.0
NCORES = 8
SEQ = 2048
D = 1024
NB = 8


class Res:
    __slots__ = ("w", "r", "dsem", "dcnt", "name", "bg")

    def __init__(self, name="", bg=False):
        self.bg = bg
        self.w = None
        self.r = {}
        self.dsem = None
        self.dcnt = 0
        self.name = name


class K:
    def __init__(self, nc, stack):
        self.nc = nc
        self.stack = stack
        self.eng = {"pe": nc.tensor, "act": nc.scalar, "dve": nc.vector, "pool": nc.gpsimd, "sp": nc.sync}
        self.esem = {k: stack.enter_context(nc.semaphore("es_" + k)) for k in self.eng}
        self.ecnt = {k: 0 for k in self.eng}
        self.waited = {k: {} for k in self.eng}
        self.dres = []
        self.nsem = 0

    def _wait(self, e, tok, kind):
        if tok is None:
            return
        sem, val = tok
        if sem is self.esem[e]:
            if e == "pe" or kind != "raw":
                return
        if self.waited[e].get(sem, 0) >= val:
            return
        self.eng[e].wait_ge(sem, val)
        self.waited[e][sem] = val

    def _pre(self, e, reads, writes):
        for r in reads:
            self._wait(e, r.w, "raw")
        for w in writes:
            self._wait(e, w.w, "waw")
            for sem, val in list(w.r.items()):
                self._wait(e, (sem, val), "war")

    def _post(self, tok, reads, writes):
        for r in reads:
            if r.r.get(tok[0], 0) < tok[1]:
                r.r[tok[0]] = tok[1]
        for w in writes:
            w.w = tok
            w.r = {}

    def op(self, e, fn, reads=(), writes=()):
        self._pre(e, reads, writes)
        ins = fn(self.eng[e])
        self.ecnt[e] += 1
        ins.then_inc(self.esem[e], 1)
        tok = (self.esem[e], self.ecnt[e])
        self._post(tok, reads, writes)
        return tok

    def dma(self, e, fn, owner, reads=(), writes=()):
        self._pre(e, reads, writes)
        if owner.dsem is None:
            owner.dsem = self.stack.enter_context(self.nc.semaphore("ds%d" % self.nsem))
            self.nsem += 1
            self.dres.append(owner)
        elif owner.dcnt > 0:
            self._wait(e, (owner.dsem, owner.dcnt), "raw")
        ins = fn(self.eng[e])
        owner.dcnt += 16
        ins.then_inc(owner.dsem, 16)
        tok = (owner.dsem, owner.dcnt)
        self._post(tok, reads, writes)
        return tok

    def barrier(self, full=False):
        for e in self.eng:
            for o in self.eng:
                if o != e and self.ecnt[o] > 0:
                    self._wait(e, (self.esem[o], self.ecnt[o]), "raw")
            for r in self.dres:
                if r.dcnt > 0 and (full or not r.bg):
                    self._wait(e, (r.dsem, r.dcnt), "raw")


def _chunk_keys():
    rows = SEQ // 64
    types = {}
    lists = []
    for rp in range(rows // 2):
        qrows = (2 * rp, 2 * rp + 1)
        rs = [min(max(r - 4, 0), rows - 8) for r in qrows]
        kc_lo = rs[0] // 2
        kc_hi = (rs[1] + 7) // 2
        lst = []
        for kc in range(kc_lo, kc_hi + 1):
            pat = tuple(tuple(rs[b] <= 2 * kc + a <= rs[b] + 7 for b in range(2)) for a in range(2))
            key = (kc - rp, pat)
            if key not in types:
                types[key] = len(types)
            lst.append((kc, types[key]))
        lists.append(lst)
    return lists, types


CHUNK_KEYS, TILE_TYPES = _chunk_keys()
NTT = len(TILE_TYPES)


def _bias_tiles(rpb):
    qcol = np.arange(64)
    kcol = np.arange(64)
    cs = np.clip(qcol - 8, 0, 48)
    col_in = (kcol[None, :] >= cs[:, None]) & (kcol[None, :] < cs[:, None] + 16)
    dc = np.clip(kcol[None, :] - qcol[:, None] + 15, 0, 30)
    out = np.full((128, 8 * NTT, 128), NEG, np.float32)
    for (o, pat), ti in TILE_TYPES.items():
        for a in range(2):
            for b in range(2):
                if not pat[a][b]:
                    continue
                dr = 2 * o + a - b
                assert -7 <= dr <= 7
                for h in range(8):
                    blk = np.where(col_in, rpb[h, dr + 7][dc], np.float32(NEG)).astype(np.float32)
                    out[b * 64:(b + 1) * 64, h * NTT + ti, a * 64:(a + 1) * 64] = blk
    return out


def build_program():
    nc = bass.Bass("TRN2", target_bir_lowering=False)
    dd = {}

    def din(name, shape, dt=F32):
        dd[name] = nc.dram_tensor(name, list(shape), dt, kind="ExternalInput").ap()
        return dd[name]

    x_d = din("x", [2 * SEQ, D])
    ctx_d = din("ctx", [512, D])
    cT_d = din("cT", [128, 8, 4])
    adaw_d = din("ada_w", [D, 6 * D])
    adab_d = din("adab_rep", [4, 6 * D])
    n1gT_d = din("n1gT", [128, 8])
    n2g_d = din("n2g_rep", [128, D])
    fg_d = din("fg_rep", [128, D])
    lng_d = din("lng_rep", [128, 512])
    win_d = din("w_in", [D, 4608])
    bias_d = din("bias_t", [128, 8 * NTT, 128])
    wsT_d = din("wsT", [128, 8, 128])
    bsT_d = din("bsT", [128, 8])
    wpa_d = din("w_pa", [512, D])
    wpb_d = din("w_pb", [512, D])
    wout_d = din("w_out", [D, D])
    wq_d = din("peer_wq", [D, 2048])
    keysT_d = din("keysT", [128, 16, 128])
    uv_d = din("peer_uv", [16384, 2 * D])
    ident_d = din("ident", [128, 128])
    iota_d = din("iota16", [128, 16])
    sel_d = din("sel", [4, 2, 128])
    out_d = nc.dram_tensor("out", [2 * SEQ, D], F32, kind="ExternalOutput").ap()

    win_v = win_d.rearrange("(kb p) c -> p kb c", p=128)
    adaw_v = adaw_d.rearrange("(kb p) c -> p kb c", p=128)

    with ExitStack() as outer:
        k = K(nc, outer)
        uid = [0]

        def sb(stack, shape, dt, nm="t"):
            uid[0] += 1
            return stack.enter_context(nc.sbuf_tensor("%s%d" % (nm, uid[0]), list(shape), dt))

        def ps(stack, nm):
            return stack.enter_context(nc.psum_tensor(nm, [128, 1024], F32))

        PA, PB, PC, PD = [ps(outer, "ps%d" % i) for i in range(4)]
        rPA, rPB, rPC, rPD = Res("PA"), Res("PB"), Res("PC"), Res("PD")

        def v8(p):
            return p[:].rearrange("p (b t) -> p b t", t=128)

        ident = sb(outer, [128, 128], F32, "ident")
        identb = sb(outer, [128, 128], BF16, "identb")
        tmp4 = sb(outer, [4, D], F32, "tmp4")
        rtmp4 = Res("tmp4")
        rowsd = nc.dram_tensor("rowsd", [4, 6 * D], F32).ap()
        rrowsd = Res("rowsd")
        A1T = sb(outer, [128, 8, 4], F32, "A1T")
        modT = sb(outer, [128, 16, 4], F32, "modT")
        n1gT = sb(outer, [128, 8], F32, "n1gT")
        sel = sb(outer, [4, 2, 128], F32, "sel")
        epsb = sb(outer, [128, 1], F32, "epsb")
        rconst = Res("const")
        rrows = Res("rows")
        rmod = Res("mod")

        def ld(dst_ap, src_ap, res, eng="sp"):
            return k.dma(eng, lambda e: e.dma_start(out=dst_ap, in_=src_ap), res, writes=[res])

        ld(ident[:], ident_d[:, :], rconst)
        ld(n1gT[:], n1gT_d[:, :], rconst)
        ld(sel[:], sel_d[:, :, :], rconst)
        k.op("pool", lambda e: e.tensor_copy(out=identb[:], in_=ident[:]), reads=[rconst], writes=[rconst])
        k.op("pool", lambda e: e.memset(epsb[:], EPS), writes=[rconst])

        with ExitStack() as s0:
            rows = sb(s0, [4, 6 * D], F32, "rows")
            cT = sb(s0, [128, 8, 4], F32, "cT")
            scT = sb(s0, [128, 8, 4], F32, "scT")
            adab = sb(s0, [4, 6 * D], F32, "adab")
            wbuf = [sb(s0, [128, 8, 1024], F32, "wbuf") for _ in range(2)]
            rwbuf = [Res("wbuf0"), Res("wbuf1")]
            rc = Res("cT")
            rab = Res("adab")
            ld(cT[:], cT_d[:, :, :], rc)
            ld(adab[:], adab_d[:, :], rab)
            k.op("act", lambda e: e.activation(out=scT[:], in_=cT[:], func=AF.Silu), reads=[rc], writes=[rc])
            for pc in range(6):
                wb, rw = wbuf[pc % 2], rwbuf[pc % 2]
                ld(wb[:], adaw_v[:, :, pc * 1024:(pc + 1) * 1024], rw)
                for half in range(2):
                    for kb in range(8):
                        k.op("pe", lambda e, kb=kb, half=half, wb=wb: e.matmul(
                            PA[0:4, half * 512:(half + 1) * 512], lhsT=scT[:, kb, :],
                            rhs=wb[:, kb, half * 512:(half + 1) * 512], start=(kb == 0), stop=(kb == 7)),
                            reads=[rc, rw], writes=[rPA])
                k.op("dve", lambda e, pc=pc: e.tensor_tensor(
                    out=rows[0:4, pc * 1024:(pc + 1) * 1024], in0=PA[0:4, :],
                    in1=adab[0:4, pc * 1024:(pc + 1) * 1024], op=ALU.add),
                    reads=[rPA, rab], writes=[rrows])
            k.dma("sp", lambda e: e.dma_start(out=rowsd[:, :], in_=rows[0:4, :]), rrowsd, reads=[rrows], writes=[rrowsd])
            for blk in range(16):
                k.op("pe", lambda e, blk=blk: e.transpose(
                    out=PB[:, blk * 4:(blk + 1) * 4], in_=rows[0:4, blk * 128:(blk + 1) * 128],
                    identity=ident[0:4, 0:4]), reads=[rrows, rconst], writes=[rPB])
            k.op("dve", lambda e: e.tensor_copy(out=modT[:].rearrange("p a b -> p (a b)"), in_=PB[:, 0:64]),
                 reads=[rPB], writes=[rmod])
            k.op("dve", lambda e: e.tensor_scalar(out=A1T[:], in0=modT[:, 8:16, :], scalar1=1.0, scalar2=None,
                                                  op0=ALU.add), reads=[rmod], writes=[rmod])
            k.op("dve", lambda e: e.tensor_tensor(out=A1T[:], in0=A1T[:],
                                                  in1=n1gT[:].unsqueeze(2).to_broadcast([128, 8, 4]),
                                                  op=ALU.mult), reads=[rmod, rconst], writes=[rmod])
            k.barrier()

        def bcast(b, col0):
            k.dma("sp", lambda e: e.dma_start(out=tmp4[0:4, :], in_=rowsd[:, col0:col0 + D]), rtmp4,
                  reads=[rrowsd], writes=[rtmp4])
            for half in range(2):
                k.op("pe", lambda e, half=half: e.matmul(
                    PC[:, half * 512:(half + 1) * 512], lhsT=sel[0:4, b, :],
                    rhs=tmp4[0:4, half * 512:(half + 1) * 512], start=True, stop=True),
                    reads=[rtmp4, rconst], writes=[rPC])

        def load_cast(dst, kbn, src_v, ncols, stg, rstg, rdst, cnt):
            for c0 in range(0, ncols, 256):
                i = cnt[0] % len(stg)
                cnt[0] += 1
                ld(stg[i][:, 0:kbn, :], src_v[:, :, c0:c0 + 256], rstg[i])
                if (c0 // 256) % 2 == 0:
                    k.op("dve", lambda e, i=i, c0=c0, kbn=kbn: e.tensor_copy(
                        out=dst[:, :, c0:c0 + 256], in_=stg[i][:, 0:kbn, :]), reads=[rstg[i]], writes=[rdst])
                else:
                    k.op("act", lambda e, i=i, c0=c0, kbn=kbn: e.copy(
                        out=dst[:, :, c0:c0 + 256], in_=stg[i][:, 0:kbn, :]), reads=[rstg[i]], writes=[rdst])

        def load_rows(dst, dcol, nkb, src_v, c0, C, stg, rstg, rdst, cnt):
            g = max(1, 2048 // C)
            for kb0 in range(0, nkb, g):
                gg_ = min(g, nkb - kb0)
                i = cnt[0] % len(stg)
                cnt[0] += 1
                sv = stg[i][:].rearrange("p a b -> p (a b)")[:, 0:gg_ * C].rearrange("p (a b) -> p a b", b=C)
                ld(sv, src_v[:, kb0:kb0 + gg_, c0:c0 + C], rstg[i])
                eng = "dve" if (cnt[0] % 2 == 0) else "act"
                if eng == "dve":
                    k.op("dve", lambda e, sv=sv, kb0=kb0, gg_=gg_: e.tensor_copy(
                        out=dst[:, kb0:kb0 + gg_, dcol:dcol + C], in_=sv), reads=[rstg[i]], writes=[rdst])
                else:
                    k.op("act", lambda e, sv=sv, kb0=kb0, gg_=gg_: e.copy(
                        out=dst[:, kb0:kb0 + gg_, dcol:dcol + C], in_=sv), reads=[rstg[i]], writes=[rdst])

        def norm_front(xin, rxin, W, pz):
            ssq, rssq, xs, rxs = W.ssqs[pz], W.rssqs[pz], W.xss[pz], W.rxss[pz]
            k.op("act", lambda e: e.activation(out=W.junk[:], in_=xin[:], func=AF.Square, accum_out=ssq[:, 0:1]),
                 reads=[rxin], writes=[W.rjunk, rssq])
            k.op("act", lambda e: e.activation(out=ssq[:, 1:2], in_=ssq[:, 0:1], func=AF.Sqrt,
                                               scale=1.0 / D, bias=epsb[:, 0:1]),
                 reads=[rssq, rconst], writes=[rssq])
            k.op("dve", lambda e: e.reciprocal(out=ssq[:, 2:3], in_=ssq[:, 1:2]), reads=[rssq], writes=[rssq])
            k.op("dve", lambda e: e.tensor_scalar(out=xs[:], in0=xin[:], scalar1=ssq[:, 2:3], scalar2=None,
                                                  op0=ALU.mult), reads=[rxin, rssq], writes=[rxs])

        def norm_back(j, W, pz):
            xs, rxs, hT, rhT = W.xss[pz], W.rxss[pz], W.hTs[pz], W.rhTs[pz]
            for blk in range(8):
                k.op("pe", lambda e, blk=blk: e.transpose(out=v8(PA)[:, blk, :], in_=xs[:, blk * 128:(blk + 1) * 128],
                                                          identity=ident[:]), reads=[rxs, rconst], writes=[rPA])
            for blk in range(8):
                if blk < 4:
                    k.op("act", lambda e, blk=blk: e.activation(
                        out=hT[:, blk, :], in_=v8(PA)[:, blk, :], func=AF.Identity,
                        scale=A1T[:, blk, j:j + 1], bias=modT[:, blk, j:j + 1]),
                        reads=[rPA, rmod], writes=[rhT])
                else:
                    k.op("dve", lambda e, blk=blk: e.tensor_scalar(
                        out=hT[:, blk, :], in0=v8(PA)[:, blk, :], scalar1=A1T[:, blk, j:j + 1],
                        scalar2=modT[:, blk, j:j + 1], op0=ALU.mult, op1=ALU.add),
                        reads=[rPA, rmod], writes=[rhT])

        class WS:
            def sel(self, pz):
                self.ssq, self.rssq = self.ssqs[pz], self.rssqs[pz]
                self.xs, self.rxs = self.xss[pz], self.rxss[pz]
                self.hT, self.rhT = self.hTs[pz], self.rhTs[pz]

        def mk_work(stack, nstg=2):
            W = WS()
            W.xin = [sb(stack, [128, D], F32, "xin") for _ in range(2)]
            W.rxin = [Res("xin0"), Res("xin1")]
            W.junk = sb(stack, [128, D], BF16, "junk")
            W.rjunk = Res("junk")
            W.ssqs = [sb(stack, [128, 4], F32, "ssq") for _ in range(2)]
            W.rssqs = [Res("ssq0"), Res("ssq1")]
            W.xss = [sb(stack, [128, D], F32, "xs") for _ in range(2)]
            W.rxss = [Res("xs0"), Res("xs1")]
            W.hTs = [sb(stack, [128, 8, 128], BF16, "hT") for _ in range(2)]
            W.rhTs = [Res("hT0"), Res("hT1")]
            W.sel(0)
            W.stg = [sb(stack, [128, 8, 256], F32, "stg") for _ in range(nstg)]
            W.rstg = [Res("stg%d" % i) for i in range(nstg)]
            W.cnt = [0]
            return W

        uvb_d = nc.dram_tensor("uvb", [16384, 2 * D], BF16).ap()
        with ExitStack() as smix:
            YT = sb(smix, [128, 16, 8, 128], BF16, "YT")
            rYT = Res("YT")
            G1 = sb(smix, [128, D], F32, "G1")
            rG1 = Res("G1")
            NST = 4
            cbf = [sb(smix, [128, 2 * D], BF16, "cbf") for _ in range(NST)]
            rcbf = [Res("cbf%d" % i, bg=True) for i in range(NST)]
            rcst = [Res("cbs%d" % i, bg=True) for i in range(NST)]
            uv_v = uv_d.rearrange("(p r) c -> p r c", r=128)
            uvb_v = uvb_d.rearrange("(p r) c -> p r c", r=128)
            for t in range(128):
                i = t % NST
                k.dma("pool", lambda e, i=i, t=t: e.dma_start(out=cbf[i][:], in_=uv_v[:, t, :]), rcbf[i], writes=[rcbf[i]])
                k.dma("pool", lambda e, i=i, t=t: e.dma_start(out=uvb_v[:, t, :], in_=cbf[i][:]), rcst[i], reads=[rcbf[i]])
            for b in range(2):
                with ExitStack() as skv:
                    KT = sb(skv, [128, 4, 2304], BF16, "KT")
                    Vaug = sb(skv, [128, 18, 8, 65], BF16, "Vaug")
                    rKT, rV = Res("KT"), Res("V")
                    k.op("dve", lambda e: e.memset(Vaug[:], 1.0), writes=[rV])
                    with ExitStack() as sA:
                        W = mk_work(sA, 4)
                        Wkv = sb(sA, [128, 8, 1024], BF16, "Wkv")
                        rW = Res("Wkv")
                        load_rows(Wkv, 0, 8, win_v, 512, 1024, W.stg, W.rstg, rW, W.cnt)
                        def frontA(t):
                            sl = t % 2
                            src = x_d[b * SEQ + t * 128:b * SEQ + (t + 1) * 128, :] if t < 16 else \
                                ctx_d[b * 256 + (t - 16) * 128:b * 256 + (t - 15) * 128, :]
                            ld(W.xin[sl][:], src, W.rxin[sl])
                            norm_front(W.xin[sl], W.rxin[sl], W, sl)

                        frontA(0)
                        norm_back(b, W, 0)
                        for t in range(18):
                            sl = t % 2
                            W.sel(sl)
                            if t + 1 < 18:
                                frontA(t + 1)
                            for hp in range(4):
                                for kb in range(8):
                                    k.op("pe", lambda e, hp=hp, kb=kb: e.matmul(
                                        v8(PB)[:, hp, :], lhsT=Wkv[:, kb, hp * 128:(hp + 1) * 128], rhs=W.hT[:, kb, :],
                                        start=(kb == 0), stop=(kb == 7)), reads=[rW, W.rhT], writes=[rPB])
                            k.op("act", lambda e, t=t: e.copy(out=KT[:, :, t * 128:(t + 1) * 128], in_=v8(PB)[:, 0:4, :]),
                                 reads=[rPB], writes=[rKT])
                            for kb in range(8):
                                k.op("pe", lambda e, kb=kb: e.matmul(
                                    PC[:, 0:512], lhsT=W.hT[:, kb, :], rhs=Wkv[:, kb, 512:1024],
                                    start=(kb == 0), stop=(kb == 7)), reads=[rW, W.rhT], writes=[rPC])
                            k.op("dve", lambda e, t=t: e.tensor_copy(
                                out=Vaug[:, t, :, 0:64], in_=PC[:, 0:512].rearrange("p (h d) -> p h d", d=64)),
                                reads=[rPC], writes=[rV])
                            if t + 1 < 18:
                                norm_back(b if t + 1 < 16 else 2, W, (t + 1) % 2)
                        k.barrier()
                    with ExitStack() as sB:
                        W = mk_work(sB)
                        W3 = sb(sB, [128, 8, 1536], BF16, "W3")
                        rW3 = Res("W3")
                        biasb = sb(sB, [128, 8 * NTT, 128], BF16, "biasb")
                        rbias = Res("bias")
                        lng = sb(sB, [128, 512], F32, "lng")
                        wsTb = sb(sB, [128, 8, 128], BF16, "wsTb")
                        bsT = sb(sB, [128, 8], F32, "bsT")
                        rsm = Res("small")
                        qpad = sb(sB, [128, 8, 128], BF16, "qpad")
                        rq = Res("qpad")
                        u = sb(sB, [128, 512], F32, "u")
                        ru = Res("u")
                        gl = sb(sB, [128, 512], F32, "gl")
                        rgl = Res("gl")
                        st6 = sb(sB, [128, 8], F32, "st6")
                        rst = Res("st6")
                        vn = sb(sB, [128, 512], BF16, "vn")
                        rvn = Res("vn")
                        tt = sb(sB, [128, 512], F32, "tt")
                        rtt = Res("tt")
                        yab = sb(sB, [128, 1024], F32, "yab")
                        rya, ryb = Res("ya"), Res("yb")
                        PT = [sb(sB, [128, 8, 128], BF16, "PT") for _ in range(2)]
                        rPT = [Res("PT0"), Res("PT1")]
                        rcp = sb(sB, [128, 8], F32, "rcp")
                        rrcp = Res("rcp")
                        load_rows(W3, 0, 8, win_v, 0, 512, W.stg, W.rstg, rW3, W.cnt)
                        load_rows(W3, 512, 8, win_v, 1536, 1024, W.stg, W.rstg, rW3, W.cnt)
                        for c0 in range(0, 8 * NTT, 16):
                            n = min(16, 8 * NTT - c0)
                            i = W.cnt[0] % len(W.stg)
                            W.cnt[0] += 1
                            sv = W.stg[i][:].rearrange("p a b -> p (a b)")[:, 0:n * 128].rearrange("p (a b) -> p a b", b=128)
                            ld(sv, bias_d[:, c0:c0 + n, :], W.rstg[i])
                            k.op("dve", lambda e, sv=sv, c0=c0, n=n: e.tensor_copy(out=biasb[:, c0:c0 + n, :], in_=sv),
                                 reads=[W.rstg[i]], writes=[rbias])
                        i = W.cnt[0] % len(W.stg)
                        W.cnt[0] += 1
                        sv = W.stg[i][:].rearrange("p a b -> p (a b)")[:, 0:1024].rearrange("p (a b) -> p a b", b=128)
                        ld(sv, wsT_d[:, :, :], W.rstg[i])
                        k.op("dve", lambda e, sv=sv: e.tensor_copy(out=wsTb[:], in_=sv), reads=[W.rstg[i]], writes=[rsm])
                        ld(lng[:], lng_d[:, :], rsm)
                        ld(bsT[:], bsT_d[:, :], rsm)
                        k.op("dve", lambda e: e.memset(qpad[:], 0.0), writes=[rq])
                        if b == 0 or True:
                            bcast(b, 2048)
                            k.op("act", lambda e: e.copy(out=G1[:], in_=PC[:, :]), reads=[rPC], writes=[rG1])
                        def frontB(cc):
                            sl = cc % 2
                            ld(W.xin[sl][:], x_d[b * SEQ + cc * 128:b * SEQ + (cc + 1) * 128, :], W.rxin[sl])
                            norm_front(W.xin[sl], W.rxin[sl], W, sl)

                        frontB(0)
                        norm_back(b, W, 0)
                        for cch in range(16):
                            sl = cch % 2
                            W.sel(sl)
                            if cch + 1 < 16:
                                frontB(cch + 1)
                            for hp in range(4):
                                for kb in range(8):
                                    k.op("pe", lambda e, hp=hp, kb=kb: e.matmul(
                                        v8(PB)[:, hp, :], lhsT=W3[:, kb, hp * 128:(hp + 1) * 128], rhs=W.hT[:, kb, :],
                                        start=(kb == 0), stop=(kb == 7)), reads=[rW3, W.rhT], writes=[rPB])
                            qv = qpad[:].rearrange("p (hp two) t -> p hp two t", two=2)
                            k.op("dve", lambda e: e.tensor_scalar(out=qv[0:64, :, 0, :], in0=v8(PB)[0:64, 0:4, :],
                                                                  scalar1=0.125, scalar2=None, op0=ALU.mult),
                                 reads=[rPB], writes=[rq])
                            k.op("dve", lambda e: e.tensor_scalar(out=qv[64:128, :, 1, :], in0=v8(PB)[64:128, 0:4, :],
                                                                  scalar1=0.125, scalar2=None, op0=ALU.mult),
                                 reads=[rPB], writes=[rq])
                            for kb in range(8):
                                k.op("pe", lambda e, kb=kb: e.matmul(
                                    PC[:, 0:512], lhsT=W.hT[:, kb, :], rhs=W3[:, kb, 512:1024],
                                    start=(kb == 0), stop=(kb == 7)), reads=[rW3, W.rhT], writes=[rPC])
                            k.op("act", lambda e: e.activation(out=u[:], in_=PC[:, 0:512], func=AF.Gelu_apprx_tanh),
                                 reads=[rPC], writes=[ru])
                            for kb in range(8):
                                k.op("pe", lambda e, kb=kb: e.matmul(
                                    PD[:, 0:512], lhsT=W.hT[:, kb, :], rhs=W3[:, kb, 1024:1536],
                                    start=(kb == 0), stop=(kb == 7)), reads=[rW3, W.rhT], writes=[rPD])
                            k.op("act", lambda e: e.activation(out=gl[:], in_=PD[:, 0:512], func=AF.Gelu_apprx_tanh),
                                 reads=[rPD], writes=[rgl])
                            k.op("dve", lambda e: e.bn_stats(out=st6[:, 0:6], in_=gl[:]), reads=[rgl], writes=[rst])
                            k.op("dve", lambda e: e.bn_aggr(out=st6[:, 6:8], in_=st6[:, 0:6]), reads=[rst], writes=[rst])
                            k.op("act", lambda e: e.activation(out=st6[:, 0:1], in_=st6[:, 7:8], func=AF.Sqrt,
                                                               scale=1.0, bias=epsb[:, 0:1]),
                                 reads=[rst, rconst], writes=[rst])
                            k.op("dve", lambda e: e.reciprocal(out=st6[:, 1:2], in_=st6[:, 0:1]), reads=[rst], writes=[rst])
                            k.op("dve", lambda e: e.tensor_scalar(out=vn[:], in0=gl[:], scalar1=st6[:, 6:7],
                                                                  scalar2=st6[:, 1:2], op0=ALU.subtract, op1=ALU.mult),
                                 reads=[rgl, rst], writes=[rvn])
                            if cch + 1 < 16:
                                norm_back(b, W, (cch + 1) % 2)
                            for g in range(8):
                                k.op("pe", lambda e, g=g: e.matmul(
                                    PD[:, 512 + g * 64:512 + (g + 1) * 64], lhsT=wsTb[:, g, :],
                                    rhs=vn[:, g * 64:(g + 1) * 64], start=True, stop=True),
                                    reads=[rsm, rvn], writes=[rPD])
                            k.op("dve", lambda e: e.tensor_tensor(out=tt[:], in0=PD[:, 512:1024], in1=lng[:], op=ALU.mult),
                                 reads=[rPD, rsm], writes=[rtt])
                            ttv = tt[:].rearrange("p (g d) -> p g d", d=64)
                            k.op("dve", lambda e: e.tensor_tensor(out=ttv, in0=ttv,
                                                                  in1=bsT[:].unsqueeze(2).to_broadcast([128, 8, 64]),
                                                                  op=ALU.add), reads=[rtt, rsm], writes=[rtt])
                            k.op("dve", lambda e: e.tensor_tensor(out=yab[:, 512:1024], in0=tt[:], in1=u[:], op=ALU.mult),
                                 reads=[rtt, ru], writes=[ryb])
                            klist = CHUNK_KEYS[cch]
                            nb = len(klist)
                            nj = nb + 2
                            def att_S(h):
                                hp = h // 2
                                S, rS = (PC, rPC) if h % 2 == 0 else (PD, rPD)
                                Sv = v8(S)
                                for j, (kc, ti) in enumerate(klist):
                                    k.op("pe", lambda e, j=j, kc=kc, Sv=Sv, hp=hp, h=h: e.matmul(
                                        Sv[:, j, :], lhsT=KT[:, hp, kc * 128:(kc + 1) * 128], rhs=qpad[:, h, :],
                                        start=True, stop=False), reads=[rKT, rq], writes=[rS])
                                    k.op("pe", lambda e, j=j, ti=ti, Sv=Sv, h=h: e.matmul(
                                        Sv[:, j, :], lhsT=biasb[:, h * NTT + ti, :], rhs=identb[:],
                                        start=False, stop=True), reads=[rbias, rconst], writes=[rS])
                                for c in range(2):
                                    k.op("pe", lambda e, c=c, Sv=Sv, hp=hp, h=h: e.matmul(
                                        Sv[:, nb + c, :], lhsT=KT[:, hp, 2048 + c * 128:2048 + (c + 1) * 128],
                                        rhs=qpad[:, h, :], start=True, stop=True), reads=[rKT, rq], writes=[rS])
                                pt, rpt = PT[h % 2], rPT[h % 2]
                                k.op("act", lambda e, pt=pt, Sv=Sv: e.activation(out=pt[:, 0:nj, :], in_=Sv[:, 0:nj, :],
                                                                                 func=AF.Exp),
                                     reads=[rS], writes=[rpt])

                            def att_PV(h):
                                pt, rpt = PT[h % 2], rPT[h % 2]
                                ocol = (h // 4) * 512 + (h % 4) * 65
                                for j in range(nj):
                                    vt = klist[j][0] if j < nb else 16 + (j - nb)
                                    k.op("pe", lambda e, j=j, vt=vt, pt=pt, h=h, ocol=ocol: e.matmul(
                                        PB[:, ocol:ocol + 65], lhsT=pt[:, j, :], rhs=Vaug[:, vt, h, :],
                                        start=(j == 0), stop=(j == nj - 1)), reads=[rpt, rV], writes=[rPB])

                            for h in range(9):
                                if h < 8:
                                    att_S(h)
                                if h >= 1:
                                    att_PV(h - 1)
                            for a in range(2):
                                Ov = PB[:, a * 512:a * 512 + 260].rearrange("p (h e) -> p h e", e=65)
                                k.op("dve", lambda e, a=a, Ov=Ov: e.reciprocal(out=rcp[:, a * 4:(a + 1) * 4], in_=Ov[:, :, 64]),
                                     reads=[rPB], writes=[rrcp])
                                k.op("dve", lambda e, a=a, Ov=Ov: e.tensor_tensor(
                                    out=yab[:, a * 256:(a + 1) * 256].rearrange("p (h d) -> p h d", d=64),
                                    in0=Ov[:, :, 0:64],
                                    in1=rcp[:, a * 4:(a + 1) * 4].unsqueeze(2).to_broadcast([128, 4, 64]), op=ALU.mult),
                                    reads=[rPB, rrcp], writes=[rya])
                            for blk in range(8):
                                k.op("pe", lambda e, blk=blk: e.transpose(out=v8(PD)[:, blk, :],
                                                                          in_=yab[:, blk * 128:(blk + 1) * 128],
                                                                          identity=ident[:]),
                                     reads=[rya, ryb, rconst], writes=[rPD])
                            k.op("dve", lambda e, cch=cch: e.tensor_copy(out=YT[:, cch, :, :], in_=v8(PD)[:, :, :]),
                                 reads=[rPD], writes=[rYT])
                        k.barrier()
                with ExitStack() as sC:
                    W = mk_work(sC, 4)
                    Wg = sb(sC, [128, 8, 2048], BF16, "Wg")
                    Wpa = sb(sC, [128, 4, 1024], BF16, "Wpa")
                    Wpb = sb(sC, [128, 4, 1024], BF16, "Wpb")
                    Wo = sb(sC, [128, 8, 1024], BF16, "Wo")
                    rWg, rWp, rWo = Res("Wg"), Res("Wp"), Res("Wo")
                    sga = sb(sC, [128, 8, 128], F32, "sga")
                    sgb = sb(sC, [128, 8, 128], F32, "sgb")
                    rsga, rsgb = Res("sga"), Res("sgb")
                    t1 = sb(sC, [128, 1024], F32, "t1")
                    t2 = sb(sC, [128, 1024], F32, "t2")
                    rt1, rt2 = Res("t1"), Res("t2")
                    mT = sb(sC, [128, 8, 128], BF16, "mT")
                    rmT = Res("mT")
                    x1 = [sb(sC, [128, 1024], F32, "x1") for _ in range(2)]
                    rx1 = [Res("x1a"), Res("x1b")]
                    load_rows(Wg, 0, 8, win_v, 2560, 2048, W.stg, W.rstg, rWg, W.cnt)
                    wout_v = wout_d.rearrange("(kb p) c -> p kb c", p=128)
                    load_rows(Wo, 0, 8, wout_v, 0, 1024, W.stg, W.rstg, rWo, W.cnt)
                    load_rows(Wpa, 0, 4, wpa_d.rearrange("(kb p) c -> p kb c", p=128), 0, 1024, W.stg, W.rstg, rWp, W.cnt)
                    load_rows(Wpb, 0, 4, wpb_d.rearrange("(kb p) c -> p kb c", p=128), 0, 1024, W.stg, W.rstg, rWp, W.cnt)
                    def frontC(cc):
                        sl = cc % 2
                        ld(W.xin[sl][:], x_d[b * SEQ + cc * 128:b * SEQ + (cc + 1) * 128, :], W.rxin[sl])
                        norm_front(W.xin[sl], W.rxin[sl], W, sl)

                    frontC(0)
                    norm_back(b, W, 0)
                    for cch in range(16):
                        sl = cch % 2
                        r0 = b * SEQ + cch * 128
                        W.sel(sl)
                        if cch + 1 < 16:
                            frontC(cch + 1)
                        for (c_off, P_, rP_, sg, rsg) in ((0, PB, rPB, sga, rsga), (1024, PC, rPC, sgb, rsgb)):
                            for ob in range(8):
                                for kb in range(8):
                                    k.op("pe", lambda e, ob=ob, kb=kb, P_=P_, c_off=c_off: e.matmul(
                                        v8(P_)[:, ob, :], lhsT=Wg[:, kb, c_off + ob * 128:c_off + (ob + 1) * 128],
                                        rhs=W.hT[:, kb, :], start=(kb == 0), stop=(kb == 7)),
                                        reads=[rWg, W.rhT], writes=[rP_])
                            k.op("act", lambda e, P_=P_, sg=sg: e.activation(out=sg[:], in_=v8(P_)[:, :, :], func=AF.Sigmoid),
                                 reads=[rP_], writes=[rsg])
                        if cch + 1 < 16:
                            norm_back(b, W, (cch + 1) % 2)
                        for (y0, P_, rP_, wt) in ((0, PD, rPD, Wpa), (4, PB, rPB, Wpb)):
                            for ob in range(8):
                                for kb in range(4):
                                    k.op("pe", lambda e, ob=ob, kb=kb, P_=P_, wt=wt, y0=y0: e.matmul(
                                        v8(P_)[:, ob, :], lhsT=wt[:, kb, ob * 128:(ob + 1) * 128],
                                        rhs=YT[:, cch, y0 + kb, :], start=(kb == 0), stop=(kb == 3)),
                                        reads=[rWp, rYT], writes=[rP_])
                        k.op("dve", lambda e: e.tensor_tensor(out=t1[:], in0=PD[:, :], in1=sga[:].rearrange("p a b -> p (a b)"),
                                                              op=ALU.mult), reads=[rPD, rsga], writes=[rt1])
                        k.op("dve", lambda e: e.tensor_tensor(out=t2[:], in0=PB[:, :], in1=sgb[:].rearrange("p a b -> p (a b)"),
                                                              op=ALU.mult), reads=[rPB, rsgb], writes=[rt2])
                        k.op("dve", lambda e: e.tensor_tensor(out=mT[:].rearrange("p a b -> p (a b)"), in0=t1[:], in1=t2[:],
                                                               op=ALU.add), reads=[rt1, rt2], writes=[rmT])
                        for half in range(2):
                            for kb in range(8):
                                k.op("pe", lambda e, half=half, kb=kb: e.matmul(
                                    PC[:, half * 512:(half + 1) * 512], lhsT=mT[:, kb, :],
                                    rhs=Wo[:, kb, half * 512:(half + 1) * 512], start=(kb == 0), stop=(kb == 7)),
                                    reads=[rmT, rWo], writes=[rPC])
                        k.op("dve", lambda e: e.tensor_tensor(out=t1[:], in0=PC[:, :], in1=G1[:], op=ALU.mult),
                             reads=[rPC, rG1], writes=[rt1])
                        k.op("dve", lambda e, sl=sl: e.tensor_tensor(out=x1[sl][:], in0=t1[:], in1=W.xin[sl][:], op=ALU.add),
                             reads=[rt1, W.rxin[sl]], writes=[rx1[sl]])
                        k.dma("sp", lambda e, sl=sl, r0=r0: e.dma_start(out=out_d[r0:r0 + 128, :], in_=x1[sl][:]),
                              rx1[sl], reads=[rx1[sl]])
                    k.barrier()
            k.barrier(full=True)

        GD = BF16
        NBUF = 12
        GRP = 4
        PF = NBUF // GRP - 1
        G_INS = 2
        with ExitStack() as sP:
            wq = sb(sP, [128, 8, 2048], BF16, "wq")
            keysT = sb(sP, [128, 16, 128], BF16, "keysT")
            rwq, rkeys = Res("wq"), Res("keys")
            fg = sb(sP, [128, D], F32, "fg")
            iota = sb(sP, [128, 16], F32, "iota")
            rpc = Res("pconst")
            _A2 = sb(sP, [128, D], F32, "A2")
            _B2 = sb(sP, [128, D], F32, "B2")
            A2 = [_A2, _A2]
            B2 = [_B2, _B2]
            G2 = [sb(sP, [128, D], F32, "G2") for _ in range(2)]
            rmod2 = Res("mod2")
            cnt = [0]
            with ExitStack() as sPs:
                wq_v = wq_d.rearrange("(kb p) c -> p kb c", p=128)
                stg = [sb(sPs, [128, 8, 256], F32, "pstg") for _ in range(4)]
                rstg = [Res("pstg%d" % i) for i in range(4)]
                n2g = sb(sPs, [128, D], F32, "n2g")
                load_rows(wq, 0, 8, wq_v, 0, 2048, stg, rstg, rwq, cnt)
                i = cnt[0] % 4
                cnt[0] += 1
                sv = stg[i][:].rearrange("p a b -> p (a b)").rearrange("p (a b) -> p a b", b=128)
                ld(sv, keysT_d[:, :, :], rstg[i])
                k.op("dve", lambda e, sv=sv: e.tensor_copy(out=keysT[:], in_=sv), reads=[rstg[i]], writes=[rkeys])
                ld(n2g[:], n2g_d[:, :], rpc)
                ld(fg[:], fg_d[:, :], rpc)
                ld(iota[:], iota_d[:, :], rpc)
                rg2 = Res("g2")
                bcast(0, 4096)
                k.op("dve", lambda e: e.scalar_tensor_tensor(out=A2[0][:], in0=PC[:, :], scalar=1.0, in1=n2g[:],
                                                             op0=ALU.add, op1=ALU.mult),
                     reads=[rPC, rpc], writes=[rmod2])
                bcast(0, 3072)
                k.op("act", lambda e: e.copy(out=B2[0][:], in_=PC[:, :]), reads=[rPC], writes=[rmod2])
                for b in range(2):
                    bcast(b, 5120)
                    k.op("act", lambda e, b=b: e.copy(out=G2[b][:], in_=PC[:, :]), reads=[rPC], writes=[rg2])
                k.barrier()
            x1t = [sb(sP, [128, D], F32, "x1t") for _ in range(2)]
            rx1t = [Res("x1t0"), Res("x1t1")]
            h2 = [sb(sP, [128, D], F32, "h2") for _ in range(2)]
            rh2 = [Res("h20"), Res("h21")]
            junkb = sb(sP, [128, D], BF16, "pjunkb")
            rjunkb = Res("pjunkb")
            junkd = sb(sP, [128, D], BF16, "pjunkd")
            rjunkd = Res("pjunkd")
            ssq = sb(sP, [128, 8], F32, "pssq")
            rssq = Res("pssq")
            ssq3 = sb(sP, [128, 8], F32, "pssq3")
            rssq3 = Res("pssq3")
            h2T = sb(sP, [128, 8, 128], BF16, "h2T")
            rh2T = Res("h2T")
            qT = sb(sP, [128, 16, 128], BF16, "qT")
            rqT = Res("qT")
            W0 = sb(sP, [128, 2048], F32, "W0")
            W1 = sb(sP, [128, 2048], F32, "W1")
            rW0, rW1 = Res("W0"), Res("W1")
            top1 = sb(sP, [128, 16, 16], F32, "top1")
            idx1 = sb(sP, [128, 16, 16], U32, "idx1")
            idx1f = sb(sP, [128, 16, 16], F32, "idx1f")
            rtop1, ridx1 = Res("top1"), Res("idx1")
            best = sb(sP, [128, 8, 16], F32, "best")
            pos = sb(sP, [128, 8, 16], U32, "pos")
            rbest, rpos = Res("best"), Res("pos")
            pa_u = sb(sP, [128, 8, 16], U32, "pa_u")
            pb_u = sb(sP, [128, 8, 16], U32, "pb_u")
            pa_f = sb(sP, [128, 8, 16], F32, "pa_f")
            pb_f = sb(sP, [128, 8, 16], F32, "pb_f")
            i0f = sb(sP, [128, 8, 16], F32, "i0f")
            i1f = sb(sP, [128, 8, 16], F32, "i1f")
            rdec = Res("dec")
            ef = sb(sP, [128, 128], F32, "ef")
            ei = [sb(sP, [128, 128], I32, "ei") for _ in range(2)]
            rei = [Res("ei0"), Res("ei1")]
            gate = [sb(sP, [128, 128], F32, "gate") for _ in range(2)]
            rgate = [Res("gate0"), Res("gate1")]
            gtmp = sb(sP, [128, 8, 16], F32, "gtmp")
            gsum = sb(sP, [128, 16], F32, "gsum")
            rgt = Res("gtmp")
            NGR = 128 // GRP
            actv = sb(sP, [128, 128], F32, "actv")
            gel = sb(sP, [128, 128], F32, "gel")
            wgt = sb(sP, [128, 128], F32, "wgt")
            ract = [Res("act%d" % g) for g in range(NGR)]
            rgel = [Res("gel%d" % g) for g in range(NGR)]
            rwgt = [Res("wgt%d" % g) for g in range(NGR)]
            Gb = [sb(sP, [128, 2 * D], GD, "Gb") for _ in range(NBUF)]
            rGb = [Res("Gb%d" % i) for i in range(NBUF)]
            NDG = 8
            dg = [sb(sP, [128, 128], GD, "dg") for _ in range(NDG)]
            rdg = [Res("dg%d" % i) for i in range(NDG)]
            identg = ident if GD == F32 else identb
            x2 = W0
            rx2 = rW0
            t3 = W1
            rt3 = rW1

            def topk16(src3, rsrc, dst_top, dst_idx, rtop, ridx, scratch3, rscr, ngrp):
                for g in range(ngrp):
                    k.op("dve", lambda e, g=g: e.max(out=dst_top[:, g, 0:8], in_=src3[:, g, :]), reads=[rsrc], writes=[rtop])
                    if g % 8 == 7:
                        yield
                for g in range(ngrp):
                    k.op("dve", lambda e, g=g: e.max_index(out=dst_idx[:, g, 0:8], in_max=dst_top[:, g, 0:8],
                                                          in_values=src3[:, g, :]), reads=[rsrc, rtop], writes=[ridx])
                    if g % 8 == 7:
                        yield
                for g in range(ngrp):
                    k.op("dve", lambda e, g=g: e.match_replace(out=scratch3[:, g, :], in_to_replace=dst_top[:, g, 0:8],
                                                              in_values=src3[:, g, :], imm_value=-1e30),
                         reads=[rsrc, rtop], writes=[rscr])
                    if g % 8 == 7:
                        yield
                for g in range(ngrp):
                    k.op("dve", lambda e, g=g: e.max(out=dst_top[:, g, 8:16], in_=scratch3[:, g, :]),
                         reads=[rscr], writes=[rtop])
                    if g % 8 == 7:
                        yield
                for g in range(ngrp):
                    k.op("dve", lambda e, g=g: e.max_index(out=dst_idx[:, g, 8:16], in_max=dst_top[:, g, 8:16],
                                                          in_values=scratch3[:, g, :]), reads=[rscr, rtop], writes=[ridx])
                    if g % 8 == 7:
                        yield

            def stage1(ci):
                b = ci // 16
                sl = ci % 2
                r0 = ci * 128
                if ci == 16:
                    ld(W1[:, 0:D], n2g_d[:, :], rW1)
                    bcast(1, 4096)
                    k.op("dve", lambda e: e.scalar_tensor_tensor(out=A2[1][:], in0=PC[:, :], scalar=1.0, in1=W1[:, 0:D],
                                                                 op0=ALU.add, op1=ALU.mult),
                         reads=[rPC, rW1], writes=[rmod2])
                    bcast(1, 3072)
                    k.op("act", lambda e: e.copy(out=B2[1][:], in_=PC[:, :]), reads=[rPC], writes=[rmod2])
                ld(x1t[sl][:], out_d[r0:r0 + 128, :], rx1t[sl])
                k.op("act", lambda e: e.activation(out=junkb[:], in_=x1t[sl][:], func=AF.Square, accum_out=ssq[:, 0:1]),
                     reads=[rx1t[sl]], writes=[rjunkb, rssq])
                k.op("act", lambda e: e.activation(out=ssq[:, 1:2], in_=ssq[:, 0:1], func=AF.Sqrt, scale=1.0 / D,
                                                   bias=epsb[:, 0:1]), reads=[rssq, rconst], writes=[rssq])
                k.op("dve", lambda e: e.reciprocal(out=ssq[:, 2:3], in_=ssq[:, 1:2]), reads=[rssq], writes=[rssq])
                k.op("dve", lambda e: e.scalar_tensor_tensor(out=h2[sl][:], in0=x1t[sl][:], scalar=ssq[:, 2:3],
                                                             in1=A2[b][:], op0=ALU.mult, op1=ALU.mult),
                     reads=[rx1t[sl], rssq, rmod2], writes=[rh2[sl]])
                k.op("dve", lambda e: e.tensor_tensor(out=h2[sl][:], in0=h2[sl][:], in1=B2[b][:], op=ALU.add),
                     reads=[rh2[sl], rmod2], writes=[rh2[sl]])
                for blk in range(8):
                    k.op("pe", lambda e, blk=blk: e.transpose(out=v8(PA)[:, blk, :],
                                                              in_=h2[sl][:, blk * 128:(blk + 1) * 128],
                                                              identity=ident[:]),
                         reads=[rh2[sl], rconst], writes=[rPA])
                k.op("act", lambda e: e.copy(out=h2T[:], in_=v8(PA)[:, :, :]), reads=[rPA], writes=[rh2T])
                yield
                for half, (P_, rP_) in enumerate(((PB, rPB), (PC, rPC))):
                    for ob in range(8):
                        gi = half * 8 + ob
                        for kb in range(8):
                            k.op("pe", lambda e, ob=ob, gi=gi, kb=kb, P_=P_: e.matmul(
                                v8(P_)[:, ob, :], lhsT=wq[:, kb, gi * 128:(gi + 1) * 128], rhs=h2T[:, kb, :],
                                start=(kb == 0), stop=(kb == 7)), reads=[rwq, rh2T], writes=[rP_])
                    k.op("act", lambda e, half=half, P_=P_: e.copy(out=qT[:, half * 8:(half + 1) * 8, :], in_=v8(P_)[:, :, :]),
                         reads=[rP_], writes=[rqT])
                    yield
                s3 = W0[:].rearrange("p (g k) -> p g k", k=128)
                s3b = W1[:].rearrange("p (g k) -> p g k", k=128)
                for half, (P_, rP_) in enumerate(((PA, rPA), (PB, rPB))):
                    for ob in range(8):
                        gi = half * 8 + ob
                        k.op("pe", lambda e, ob=ob, gi=gi, P_=P_: e.matmul(
                            v8(P_)[:, ob, :], lhsT=qT[:, gi, :], rhs=keysT[:, gi, :], start=True, stop=True),
                            reads=[rqT, rkeys], writes=[rP_])
                    k.op("act", lambda e, half=half, P_=P_: e.copy(out=s3[:, half * 8:(half + 1) * 8, :], in_=v8(P_)[:, :, :]),
                         reads=[rP_], writes=[rW0])
                yield
                yield from topk16(s3, rW0, top1, idx1, rtop1, ridx1, s3b, rW1, 16)
                t1v = top1[:].rearrange("p (h two) k -> p h two k", two=2)
                cand = W0[:].rearrange("p (h a b) -> p h a b", a=16, b=16)
                k.op("dve", lambda e: e.tensor_tensor(
                    out=cand, in0=t1v[:, :, 0, :].unsqueeze(3).to_broadcast([128, 8, 16, 16]),
                    in1=t1v[:, :, 1, :].unsqueeze(2).to_broadcast([128, 8, 16, 16]), op=ALU.add),
                    reads=[rtop1], writes=[rW0])
                c3 = W0[:].rearrange("p (h c) -> p h c", c=256)
                c3b = W1[:].rearrange("p (h c) -> p h c", c=256)
                yield
                yield from topk16(c3, rW0, best, pos, rbest, rpos, c3b, rW1, 8)
                k.op("dve", lambda e: e.tensor_single_scalar(out=pa_u[:], in_=pos[:], scalar=4, op=ALU.arith_shift_right),
                     reads=[rpos], writes=[rdec])
                k.op("dve", lambda e: e.tensor_single_scalar(out=pb_u[:], in_=pos[:], scalar=15, op=ALU.bitwise_and),
                     reads=[rpos], writes=[rdec])
                k.op("dve", lambda e: e.tensor_copy(out=pa_f[:], in_=pa_u[:]), reads=[rdec], writes=[rdec])
                k.op("dve", lambda e: e.tensor_copy(out=pb_f[:], in_=pb_u[:]), reads=[rdec], writes=[rdec])
                k.op("dve", lambda e: e.tensor_copy(out=idx1f[:], in_=idx1[:]), reads=[ridx1], writes=[ridx1])
                yield
                i1v = idx1f[:].rearrange("p (h two) k -> p h two k", two=2)
                oh = W1[:].rearrange("p (h a b) -> p h a b", a=16, b=16)
                for (pf, half, dst) in ((pa_f, 0, i0f), (pb_f, 1, i1f)):
                    k.op("dve", lambda e, pf=pf: e.tensor_tensor(
                        out=oh, in0=pf[:].unsqueeze(3).to_broadcast([128, 8, 16, 16]),
                        in1=iota[:].unsqueeze(1).unsqueeze(1).to_broadcast([128, 8, 16, 16]), op=ALU.is_equal),
                        reads=[rdec, rpc], writes=[rW1])
                    k.op("dve", lambda e, half=half: e.tensor_tensor(
                        out=oh, in0=oh, in1=i1v[:, :, half, :].unsqueeze(2).to_broadcast([128, 8, 16, 16]), op=ALU.mult),
                        reads=[rW1, ridx1], writes=[rW1])
                    k.op("dve", lambda e, dst=dst: e.tensor_reduce(out=dst[:], in_=oh, axis=AX.X, op=ALU.add),
                         reads=[rW1], writes=[rdec])
                    yield
                k.op("dve", lambda e: e.scalar_tensor_tensor(
                    out=ef[:], in0=i0f[:].rearrange("p h k -> p (h k)"), scalar=128.0,
                    in1=i1f[:].rearrange("p h k -> p (h k)"), op0=ALU.mult, op1=ALU.add), reads=[rdec], writes=[rdec])
                k.op("dve", lambda e: e.tensor_copy(out=ei[sl][:], in_=ef[:]), reads=[rdec], writes=[rei[sl]])
                k.op("dve", lambda e: e.tensor_tensor(out=gtmp[:], in0=best[:],
                                                      in1=best[:, :, 0:1].to_broadcast([128, 8, 16]), op=ALU.subtract),
                     reads=[rbest], writes=[rgt])
                k.op("act", lambda e: e.activation(out=gtmp[:], in_=gtmp[:], func=AF.Exp), reads=[rgt], writes=[rgt])
                k.op("dve", lambda e: e.tensor_reduce(out=gsum[:, 0:8], in_=gtmp[:], axis=AX.X, op=ALU.add),
                     reads=[rgt], writes=[rgt])
                k.op("dve", lambda e: e.reciprocal(out=gsum[:, 8:16], in_=gsum[:, 0:8]), reads=[rgt], writes=[rgt])
                k.op("dve", lambda e: e.tensor_tensor(out=gate[sl][:].rearrange("p (h k) -> p h k", k=16), in0=gtmp[:],
                                                      in1=gsum[:, 8:16].unsqueeze(2).to_broadcast([128, 8, 16]),
                                                      op=ALU.mult), reads=[rgt], writes=[rgate[sl]])

            def gathers(gg):
                ci, g = divmod(gg, NGR)
                sl = ci % 2
                for kk in range(GRP):
                    s = g * GRP + kk
                    bi = (gg * GRP + kk) % NBUF
                    k.dma("pool", lambda e, s=s, bi=bi: e.indirect_dma_start(
                        out=Gb[bi][:], out_offset=None, in_=uv_src[:, :],
                        in_offset=bass.IndirectOffsetOnAxis(ap=ei[sl][:, s:s + 1], axis=0)),
                        rGb[bi], reads=[rei[sl]], writes=[rGb[bi]])

            def compute_a(gg):
                ci, g = divmod(gg, NGR)
                sl = ci % 2
                c0, c1 = g * GRP, (g + 1) * GRP
                for kk in range(GRP):
                    s = g * GRP + kk
                    bi = (gg * GRP + kk) % NBUF
                    k.op("dve", lambda e, s=s, bi=bi: e.scalar_tensor_tensor(
                        out=junkd[:], in0=Gb[bi][:, 0:D], scalar=1.0, in1=h2[sl][:], op0=ALU.mult, op1=ALU.mult,
                        accum_out=actv[:, s:s + 1]), reads=[rGb[bi], rh2[sl]], writes=[rjunkd, ract[g]])
                k.op("act", lambda e: e.activation(out=gel[:, c0:c1], in_=actv[:, c0:c1], func=AF.Gelu_apprx_tanh),
                     reads=[ract[g]], writes=[rgel[g]])

            def compute_b(gg):
                ci, g = divmod(gg, NGR)
                sl = ci % 2
                c0, c1 = g * GRP, (g + 1) * GRP
                k.op("dve", lambda e: e.tensor_tensor(out=wgt[:, c0:c1], in0=gel[:, c0:c1], in1=gate[sl][:, c0:c1],
                                                      op=ALU.mult), reads=[rgel[g], rgate[sl]], writes=[rwgt[g]])
                for kk in range(GRP):
                    s = g * GRP + kk
                    bi = (gg * GRP + kk) % NBUF
                    di = (gg * GRP + kk) % NDG
                    k.op("act", lambda e, s=s, di=di: e.activation(out=dg[di][:], in_=identg[:], func=AF.Copy,
                                                                  scale=wgt[:, s:s + 1]),
                         reads=[rwgt[g], rconst], writes=[rdg[di]])
                    for half in range(2):
                        k.op("pe", lambda e, s=s, bi=bi, di=di, half=half: e.matmul(
                            PD[:, half * 512:(half + 1) * 512], lhsT=dg[di][:],
                            rhs=Gb[bi][:, D + half * 512:D + (half + 1) * 512],
                            start=(s == 0), stop=(s == 127)), reads=[rdg[di], rGb[bi]], writes=[rPD])

            def finalize(ci):
                b = ci // 16
                sl = ci % 2
                r0 = ci * 128
                k.op("dve", lambda e: e.tensor_tensor(out=t3[:, 0:D], in0=PD[:, :], in1=G2[b][:], op=ALU.mult),
                     reads=[rPD, rg2], writes=[rt3])
                k.op("dve", lambda e: e.tensor_tensor(out=x2[:, 0:D], in0=t3[:, 0:D], in1=x1t[sl][:], op=ALU.add),
                     reads=[rt3, rx1t[sl]], writes=[rx2])
                k.op("act", lambda e: e.activation(out=junkb[:], in_=x2[:, 0:D], func=AF.Square, accum_out=ssq3[:, 0:1]),
                     reads=[rx2], writes=[rjunkb, rssq3])
                k.op("act", lambda e: e.activation(out=ssq3[:, 1:2], in_=ssq3[:, 0:1], func=AF.Sqrt, scale=1.0 / D,
                                                   bias=epsb[:, 0:1]), reads=[rssq3, rconst], writes=[rssq3])
                k.op("dve", lambda e: e.reciprocal(out=ssq3[:, 2:3], in_=ssq3[:, 1:2]), reads=[rssq3], writes=[rssq3])
                k.op("dve", lambda e: e.scalar_tensor_tensor(out=x2[:, 0:D], in0=x2[:, 0:D], scalar=ssq3[:, 2:3], in1=fg[:],
                                                             op0=ALU.mult, op1=ALU.mult),
                     reads=[rx2, rssq3, rpc], writes=[rx2])
                k.dma("sp", lambda e: e.dma_start(out=out_d[r0:r0 + 128, :], in_=x2[:, 0:D]), rx2, reads=[rx2])

            uv_src = uvb_d
            NCH = 32
            TOT = NCH * NGR
            for _ in stage1(0):
                pass
            for gg in range(min(PF, TOT)):
                gathers(gg)
            gen = None
            for gg in range(TOT):
                ci, g = divmod(gg, NGR)
                if g == G_INS and ci + 1 < NCH:
                    gen = stage1(ci + 1)
                if gg + PF < TOT:
                    gathers(gg + PF)
                compute_a(gg)
                if gen is not None:
                    if g >= NGR - PF - 1:
                        for _ in gen:
                            pass
                        gen = None
                    elif next(gen, "done") == "done":
                        gen = None
                compute_b(gg)
                if g == NGR - 1:
                    finalize(ci)
            k.barrier()
    return nc


_NC_CACHE = {}


def kernel(x, c, ctx, c_ctx, ada_w, ada_b, norm1_g, norm2_g, w_in, na_rpb, gm_ln_g, gm_ws, gm_bs,
           w_proj_a, w_proj_b, w_out, peer_wq, peer_keys, peer_u, peer_v, final_g):
    f = lambda a: np.ascontiguousarray(np.asarray(a, dtype=np.float32))
    x, c, ctx, c_ctx = f(x), f(c), f(ctx), f(c_ctx)
    if "nc" not in _NC_CACHE:
        _NC_CACHE["nc"] = build_program()
    nc = _NC_CACHE["nc"]
    shared = {
        "ada_w": f(ada_w)[0],
        "adab_rep": f(np.broadcast_to(f(ada_b)[0][None, :], (4, 6 * D))),
        "n1gT": f(f(norm1_g)[0].reshape(8, 128).T),
        "n2g_rep": f(np.broadcast_to(f(norm2_g)[0][None, :], (128, D))),
        "fg_rep": f(np.broadcast_to(f(final_g)[None, :], (128, D))),
        "lng_rep": f(np.broadcast_to(f(gm_ln_g)[0][None, :], (128, 512))),
        "w_in": f(w_in)[0],
        "bias_t": _bias_tiles(f(na_rpb)[0]),
        "wsT": f(np.transpose(f(gm_ws)[0], (2, 0, 1))),
        "bsT": f(f(gm_bs)[0].T),
        "w_pa": f(w_proj_a)[0],
        "w_pb": f(w_proj_b)[0],
        "w_out": f(w_out)[0],
        "peer_wq": f(peer_wq)[0],
        "keysT": f(np.transpose(f(peer_keys)[0].reshape(16, 128, 128), (2, 0, 1))),
        "peer_uv": f(np.concatenate([f(peer_u)[0], f(peer_v)[0]], axis=1)),
        "ident": np.eye(128, dtype=np.float32),
        "iota16": f(np.broadcast_to(np.arange(16, dtype=np.float32)[None, :], (128, 16))),
        "sel": f(np.stack([np.stack([np.full(128, 1.0 if kk == bb else 0.0, np.float32) for bb in range(2)])
                           for kk in range(4)])),
    }
    in_maps = []
    for core in range(NCORES):
        b0 = 2 * core
        cv = np.stack([c[b0], c[b0 + 1], c_ctx, c_ctx], axis=1)
        m = dict(shared)
        m["x"] = f(x[b0:b0 + 2].reshape(2 * SEQ, D))
        m["ctx"] = f(ctx[b0:b0 + 2].reshape(512, D))
        m["cT"] = f(cv.reshape(8, 128, 4).transpose(1, 0, 2))
        in_maps.append(m)
    res = run_bass_kernel_spmd(nc, in_maps, core_ids=list(range(NCORES)))
    outs = [np.asarray(r["out"], dtype=np.float32).reshape(2, SEQ, D) for r in res.results]
    return np.concatenate(outs, axis=0)
```

```python
import numpy as np
from contextlib import ExitStack
import concourse.bass as bass
import concourse.mybir as mybir
from concourse.bass_utils import run_bass_kernel_spmd

F32 = mybir.dt.float32
BF16 = mybir.dt.bfloat16
I32 = mybir.dt.int32
U32 = mybir.dt.uint32
AF = mybir.ActivationFunctionType
ALU = mybir.AluOpType
AX = mybir.AxisListType
EPS = 1e-6
NEG = -30000.0
NCORES = 8
SEQ = 2048
D = 1024
NB = 8


class Res:
    __slots__ = ("w", "r", "dsem", "dcnt", "name", "bg")

    def __init__(self, name="", bg=False):
        self.bg = bg
        self.w = None
        self.r = {}
        self.dsem = None
        self.dcnt = 0
        self.name = name


class K:
    def __init__(self, nc, stack):
        self.nc = nc
        self.stack = stack
        self.eng = {"pe": nc.tensor, "act": nc.scalar, "dve": nc.vector, "pool": nc.gpsimd, "sp": nc.sync}
        self.esem = {k: stack.enter_context(nc.semaphore("es_" + k)) for k in self.eng}
        self.ecnt = {k: 0 for k in self.eng}
        self.waited = {k: {} for k in self.eng}
        self.dres = []
        self.nsem = 0

    def _wait(self, e, tok, kind):
        if tok is None:
            return
        sem, val = tok
        if sem is self.esem[e]:
            if e == "pe" or kind != "raw":
                return
        if self.waited[e].get(sem, 0) >= val:
            return
        self.eng[e].wait_ge(sem, val)
        self.waited[e][sem] = val

    def _pre(self, e, reads, writes):
        for r in reads:
            self._wait(e, r.w, "raw")
        for w in writes:
            self._wait(e, w.w, "waw")
            for sem, val in list(w.r.items()):
                self._wait(e, (sem, val), "war")

    def _post(self, tok, reads, writes):
        for r in reads:
            if r.r.get(tok[0], 0) < tok[1]:
                r.r[tok[0]] = tok[1]
        for w in writes:
            w.w = tok
            w.r = {}

    def op(self, e, fn, reads=(), writes=()):
        self._pre(e, reads, writes)
        ins = fn(self.eng[e])
        self.ecnt[e] += 1
        ins.then_inc(self.esem[e], 1)
        tok = (self.esem[e], self.ecnt[e])
        self._post(tok, reads, writes)
        return tok

    def dma(self, e, fn, owner, reads=(), writes=()):
        self._pre(e, reads, writes)
        if owner.dsem is None:
            owner.dsem = self.stack.enter_context(self.nc.semaphore("ds%d" % self.nsem))
            self.nsem += 1
            self.dres.append(owner)
        elif owner.dcnt > 0:
            self._wait(e, (owner.dsem, owner.dcnt), "raw")
        ins = fn(self.eng[e])
        owner.dcnt += 16
        ins.then_inc(owner.dsem, 16)
        tok = (owner.dsem, owner.dcnt)
        self._post(tok, reads, writes)
        return tok

    def barrier(self, full=False):
        for e in self.eng:
            for o in self.eng:
                if o != e and self.ecnt[o] > 0:
                    self._wait(e, (self.esem[o], self.ecnt[o]), "raw")
            for r in self.dres:
                if r.dcnt > 0 and (full or not r.bg):
                    self._wait(e, (r.dsem, r.dcnt), "raw")


def _chunk_keys():
    rows = SEQ // 64
    types = {}
    lists = []
    for rp in range(rows // 2):
        qrows = (2 * rp, 2 * rp + 1)
        rs = [min(max(r - 4, 0), rows - 8) for r in qrows]
        kc_lo = rs[0] // 2
        kc_hi = (rs[1] + 7) // 2
        lst = []
        for kc in range(kc_lo, kc_hi + 1):
            pat = tuple(tuple(rs[b] <= 2 * kc + a <= rs[b] + 7 for b in range(2)) for a in range(2))
            key = (kc - rp, pat)
            if key not in types:
                types[key] = len(types)
            lst.append((kc, types[key]))
        lists.append(lst)
    return lists, types


CHUNK_KEYS, TILE_TYPES = _chunk_keys()
NTT = len(TILE_TYPES)


def _bias_tiles(rpb):
    qcol = np.arange(64)
    kcol = np.arange(64)
    cs = np.clip(qcol - 8, 0, 48)
    col_in = (kcol[None, :] >= cs[:, None]) & (kcol[None, :] < cs[:, None] + 16)
    dc = np.clip(kcol[None, :] - qcol[:, None] + 15, 0, 30)
    out = np.full((128, 8 * NTT, 128), NEG, np.float32)
    for (o, pat), ti in TILE_TYPES.items():
        for a in range(2):
            for b in range(2):
                if not pat[a][b]:
                    continue
                dr = 2 * o + a - b
                assert -7 <= dr <= 7
                for h in range(8):
                    blk = np.where(col_in, rpb[h, dr + 7][dc], np.float32(NEG)).astype(np.float32)
                    out[b * 64:(b + 1) * 64, h * NTT + ti, a * 64:(a + 1) * 64] = blk
    return out


def build_program():
    nc = bass.Bass("TRN2", target_bir_lowering=False)
    dd = {}

    def din(name, shape, dt=F32):
        dd[name] = nc.dram_tensor(name, list(shape), dt, kind="ExternalInput").ap()
        return dd[name]

    x_d = din("x", [2 * SEQ, D])
    ctx_d = din("ctx", [512, D])
    cT_d = din("cT", [128, 8, 4])
    adaw_d = din("ada_w", [D, 6 * D])
    adab_d = din("adab_rep", [4, 6 * D])
    n1gT_d = din("n1gT", [128, 8])
    n2g_d = din("n2g_rep", [128, D])
    fg_d = din("fg_rep", [128, D])
    lng_d = din("lng_rep", [128, 512])
    win_d = din("w_in", [D, 4608])
    bias_d = din("bias_t", [128, 8 * NTT, 128])
    wsT_d = din("wsT", [128, 8, 128])
    bsT_d = din("bsT", [128, 8])
    wpa_d = din("w_pa", [512, D])
    wpb_d = din("w_pb", [512, D])
    wout_d = din("w_out", [D, D])
    wq_d = din("peer_wq", [D, 2048])
    keysT_d = din("keysT", [128, 16, 128])
    uv_d = din("peer_uv", [16384, 2 * D])
    ident_d = din("ident", [128, 128])
    iota_d = din("iota16", [128, 16])
    sel_d = din("sel", [4, 2, 128])
    out_d = nc.dram_tensor("out", [2 * SEQ, D], F32, kind="ExternalOutput").ap()

    win_v = win_d.rearrange("(kb p) c -> p kb c", p=128)
    adaw_v = adaw_d.rearrange("(kb p) c -> p kb c", p=128)

    with ExitStack() as outer:
        k = K(nc, outer)
        uid = [0]

        def sb(stack, shape, dt, nm="t"):
            uid[0] += 1
            return stack.enter_context(nc.sbuf_tensor("%s%d" % (nm, uid[0]), list(shape), dt))

        def ps(stack, nm):
            return stack.enter_context(nc.psum_tensor(nm, [128, 1024], F32))

        PA, PB, PC, PD = [ps(outer, "ps%d" % i) for i in range(4)]
        rPA, rPB, rPC, rPD = Res("PA"), Res("PB"), Res("PC"), Res("PD")

        def v8(p):
            return p[:].rearrange("p (b t) -> p b t", t=128)

        ident = sb(outer, [128, 128], F32, "ident")
        identb = sb(outer, [128, 128], BF16, "identb")
        tmp4 = sb(outer, [4, D], F32, "tmp4")
        rtmp4 = Res("tmp4")
        rowsd = nc.dram_tensor("rowsd", [4, 6 * D], F32).ap()
        rrowsd = Res("rowsd")
        A1T = sb(outer, [128, 8, 4], F32, "A1T")
        modT = sb(outer, [128, 16, 4], F32, "modT")
        n1gT = sb(outer, [128, 8], F32, "n1gT")
        sel = sb(outer, [4, 2, 128], F32, "sel")
        epsb = sb(outer, [128, 1], F32, "epsb")
        rconst = Res("const")
        rrows = Res("rows")
        rmod = Res("mod")

        def ld(dst_ap, src_ap, res, eng="sp"):
            return k.dma(eng, lambda e: e.dma_start(out=dst_ap, in_=src_ap), res, writes=[res])

        ld(ident[:], ident_d[:, :], rconst)
        ld(n1gT[:], n1gT_d[:, :], rconst)
        ld(sel[:], sel_d[:, :, :], rconst)
        k.op("pool", lambda e: e.tensor_copy(out=identb[:], in_=ident[:]), reads=[rconst], writes=[rconst])
        k.op("pool", lambda e: e.memset(epsb[:], EPS), writes=[rconst])

        with ExitStack() as s0:
            rows = sb(s0, [4, 6 * D], F32, "rows")
            cT = sb(s0, [128, 8, 4], F32, "cT")
            scT = sb(s0, [128, 8, 4], F32, "scT")
            adab = sb(s0, [4, 6 * D], F32, "adab")
            wbuf = [sb(s0, [128, 8, 1024], F32, "wbuf") for _ in range(2)]
            rwbuf = [Res("wbuf0"), Res("wbuf1")]
            rc = Res("cT")
            rab = Res("adab")
            ld(cT[:], cT_d[:, :, :], rc)
            ld(adab[:], adab_d[:, :], rab)
            k.op("act", lambda e: e.activation(out=scT[:], in_=cT[:], func=AF.Silu), reads=[rc], writes=[rc])
            for pc in range(6):
                wb, rw = wbuf[pc % 2], rwbuf[pc % 2]
                ld(wb[:], adaw_v[:, :, pc * 1024:(pc + 1) * 1024], rw)
                for half in range(2):
                    for kb in range(8):
                        k.op("pe", lambda e, kb=kb, half=half, wb=wb: e.matmul(
                            PA[0:4, half * 512:(half + 1) * 512], lhsT=scT[:, kb, :],
                            rhs=wb[:, kb, half * 512:(half + 1) * 512], start=(kb == 0), stop=(kb == 7)),
                            reads=[rc, rw], writes=[rPA])
                k.op("dve", lambda e, pc=pc: e.tensor_tensor(
                    out=rows[0:4, pc * 1024:(pc + 1) * 1024], in0=PA[0:4, :],
                    in1=adab[0:4, pc * 1024:(pc + 1) * 1024], op=ALU.add),
                    reads=[rPA, rab], writes=[rrows])
            k.dma("sp", lambda e: e.dma_start(out=rowsd[:, :], in_=rows[0:4, :]), rrowsd, reads=[rrows], writes=[rrowsd])
            for blk in range(16):
                k.op("pe", lambda e, blk=blk: e.transpose(
                    out=PB[:, blk * 4:(blk + 1) * 4], in_=rows[0:4, blk * 128:(blk + 1) * 128],
                    identity=ident[0:4, 0:4]), reads=[rrows, rconst], writes=[rPB])
            k.op("dve", lambda e: e.tensor_copy(out=modT[:].rearrange("p a b -> p (a b)"), in_=PB[:, 0:64]),
                 reads=[rPB], writes=[rmod])
            k.op("dve", lambda e: e.tensor_scalar(out=A1T[:], in0=modT[:, 8:16, :], scalar1=1.0, scalar2=None,
                                                  op0=ALU.add), reads=[rmod], writes=[rmod])
            k.op("dve", lambda e: e.tensor_tensor(out=A1T[:], in0=A1T[:],
                                                  in1=n1gT[:].unsqueeze(2).to_broadcast([128, 8, 4]),
                                                  op=ALU.mult), reads=[rmod, rconst], writes=[rmod])
            k.barrier()

        def bcast(b, col0):
            k.dma("sp", lambda e: e.dma_start(out=tmp4[0:4, :], in_=rowsd[:, col0:col0 + D]), rtmp4,
                  reads=[rrowsd], writes=[rtmp4])
            for half in range(2):
                k.op("pe", lambda e, half=half: e.matmul(
                    PC[:, half * 512:(half + 1) * 512], lhsT=sel[0:4, b, :],
                    rhs=tmp4[0:4, half * 512:(half + 1) * 512], start=True, stop=True),
                    reads=[rtmp4, rconst], writes=[rPC])

        def load_cast(dst, kbn, src_v, ncols, stg, rstg, rdst, cnt):
            for c0 in range(0, ncols, 256):
                i = cnt[0] % len(stg)
                cnt[0] += 1
                ld(stg[i][:, 0:kbn, :], src_v[:, :, c0:c0 + 256], rstg[i])
                if (c0 // 256) % 2 == 0:
                    k.op("dve", lambda e, i=i, c0=c0, kbn=kbn: e.tensor_copy(
                        out=dst[:, :, c0:c0 + 256], in_=stg[i][:, 0:kbn, :]), reads=[rstg[i]], writes=[rdst])
                else:
                    k.op("act", lambda e, i=i, c0=c0, kbn=kbn: e.copy(
                        out=dst[:, :, c0:c0 + 256], in_=stg[i][:, 0:kbn, :]), reads=[rstg[i]], writes=[rdst])

        def load_rows(dst, dcol, nkb, src_v, c0, C, stg, rstg, rdst, cnt):
            for _ in load_rows_gen(dst, dcol, nkb, src_v, c0, C, stg, rstg, rdst, cnt):
                pass

        def load_rows_gen(dst, dcol, nkb, src_v, c0, C, stg, rstg, rdst, cnt):
            g = max(1, 2048 // C)
            for kb0 in range(0, nkb, g):
                gg_ = min(g, nkb - kb0)
                i = cnt[0] % len(stg)
                cnt[0] += 1
                sv = stg[i][:].rearrange("p a b -> p (a b)")[:, 0:gg_ * C].rearrange("p (a b) -> p a b", b=C)
                ld(sv, src_v[:, kb0:kb0 + gg_, c0:c0 + C], rstg[i])
                eng = "dve" if (cnt[0] % 2 == 0) else "act"
                if eng == "dve":
                    k.op("dve", lambda e, sv=sv, kb0=kb0, gg_=gg_: e.tensor_copy(
                        out=dst[:, kb0:kb0 + gg_, dcol:dcol + C], in_=sv), reads=[rstg[i]], writes=[rdst])
                else:
                    k.op("act", lambda e, sv=sv, kb0=kb0, gg_=gg_: e.copy(
                        out=dst[:, kb0:kb0 + gg_, dcol:dcol + C], in_=sv), reads=[rstg[i]], writes=[rdst])
                yield

        def norm_front(xin, rxin, W, pz):
            ssq, rssq, xs, rxs = W.ssqs[pz], W.rssqs[pz], W.xss[pz], W.rxss[pz]
            k.op("act", lambda e: e.activation(out=W.junk[:], in_=xin[:], func=AF.Square, accum_out=ssq[:, 0:1]),
                 reads=[rxin], writes=[W.rjunk, rssq])
            k.op("act", lambda e: e.activation(out=ssq[:, 1:2], in_=ssq[:, 0:1], func=AF.Sqrt,
                                               scale=1.0 / D, bias=epsb[:, 0:1]),
                 reads=[rssq, rconst], writes=[rssq])
            k.op("dve", lambda e: e.reciprocal(out=ssq[:, 2:3], in_=ssq[:, 1:2]), reads=[rssq], writes=[rssq])
            k.op("dve", lambda e: e.tensor_scalar(out=xs[:], in0=xin[:], scalar1=ssq[:, 2:3], scalar2=None,
                                                  op0=ALU.mult), reads=[rxin, rssq], writes=[rxs])

        def norm_back(j, W, pz):
            xs, rxs, hT, rhT = W.xss[pz], W.rxss[pz], W.hTs[pz], W.rhTs[pz]
            for blk in range(8):
                k.op("pe", lambda e, blk=blk: e.transpose(out=v8(PA)[:, blk, :], in_=xs[:, blk * 128:(blk + 1) * 128],
                                                          identity=ident[:]), reads=[rxs, rconst], writes=[rPA])
            for blk in range(8):
                k.op("act", lambda e, blk=blk: e.activation(
                    out=hT[:, blk, :], in_=v8(PA)[:, blk, :], func=AF.Identity,
                    scale=A1T[:, blk, j:j + 1], bias=modT[:, blk, j:j + 1]),
                    reads=[rPA, rmod], writes=[rhT])

        class WS:
            def sel(self, pz):
                self.ssq, self.rssq = self.ssqs[pz], self.rssqs[pz]
                self.xs, self.rxs = self.xss[pz], self.rxss[pz]
                self.hT, self.rhT = self.hTs[pz], self.rhTs[pz]

        def mk_work(stack, nstg=2):
            W = WS()
            W.xin = [sb(stack, [128, D], F32, "xin") for _ in range(2)]
            W.rxin = [Res("xin0"), Res("xin1")]
            W.junk = sb(stack, [128, D], BF16, "junk")
            W.rjunk = Res("junk")
            W.ssqs = [sb(stack, [128, 4], F32, "ssq") for _ in range(2)]
            W.rssqs = [Res("ssq0"), Res("ssq1")]
            W.xss = [sb(stack, [128, D], F32, "xs") for _ in range(2)]
            W.rxss = [Res("xs0"), Res("xs1")]
            W.hTs = [sb(stack, [128, 8, 128], BF16, "hT") for _ in range(2)]
            W.rhTs = [Res("hT0"), Res("hT1")]
            W.sel(0)
            W.stg = [sb(stack, [128, 8, 256], F32, "stg") for _ in range(nstg)]
            W.rstg = [Res("stg%d" % i) for i in range(nstg)]
            W.cnt = [0]
            return W

        uvb_d = nc.dram_tensor("uvb", [16384, 2 * D], BF16).ap()
        with ExitStack() as smix:
            YT = sb(smix, [128, 16, 8, 128], BF16, "YT")
            rYT = Res("YT")
            G1 = sb(smix, [128, D], F32, "G1")
            rG1 = Res("G1")
            NST = 4
            cbf = [sb(smix, [128, 2 * D], BF16, "cbf") for _ in range(NST)]
            rcbf = [Res("cbf%d" % i, bg=True) for i in range(NST)]
            rcst = [Res("cbs%d" % i, bg=True) for i in range(NST)]
            uv_v = uv_d.rearrange("(p r) c -> p r c", r=128)
            uvb_v = uvb_d.rearrange("(p r) c -> p r c", r=128)
            for t in range(128):
                i = t % NST
                k.dma("pool", lambda e, i=i, t=t: e.dma_start(out=cbf[i][:], in_=uv_v[:, t, :]), rcbf[i], writes=[rcbf[i]])
                k.dma("pool", lambda e, i=i, t=t: e.dma_start(out=uvb_v[:, t, :], in_=cbf[i][:]), rcst[i], reads=[rcbf[i]])
            for b in range(2):
                with ExitStack() as skv:
                    KT = sb(skv, [128, 4, 2304], BF16, "KT")
                    Vaug = sb(skv, [128, 18, 8, 65], BF16, "Vaug")
                    rKT, rV = Res("KT"), Res("V")
                    k.op("dve", lambda e: e.memset(Vaug[:], 1.0), writes=[rV])
                    W3 = sb(skv, [128, 8, 1536], BF16, "W3")
                    rW3 = Res("W3")
                    biasb = sb(skv, [128, 8 * NTT, 128], BF16, "biasb")
                    rbias = Res("bias")
                    lng = sb(skv, [128, 512], F32, "lng")
                    wsTb = sb(skv, [128, 8, 128], BF16, "wsTb")
                    bsT = sb(skv, [128, 8], F32, "bsT")
                    rsm = Res("small")

                    def b1_weights(W):
                        yield from load_rows_gen(W3, 0, 8, win_v, 0, 512, W.stg, W.rstg, rW3, W.cnt)
                        yield from load_rows_gen(W3, 512, 8, win_v, 1536, 1024, W.stg, W.rstg, rW3, W.cnt)
                        for c0 in range(0, 8 * NTT, 16):
                            n = min(16, 8 * NTT - c0)
                            i = W.cnt[0] % len(W.stg)
                            W.cnt[0] += 1
                            sv = W.stg[i][:].rearrange("p a b -> p (a b)")[:, 0:n * 128].rearrange("p (a b) -> p a b", b=128)
                            ld(sv, bias_d[:, c0:c0 + n, :], W.rstg[i])
                            k.op("dve", lambda e, sv=sv, c0=c0, n=n: e.tensor_copy(out=biasb[:, c0:c0 + n, :], in_=sv),
                                 reads=[W.rstg[i]], writes=[rbias])
                            yield
                        i = W.cnt[0] % len(W.stg)
                        W.cnt[0] += 1
                        sv = W.stg[i][:].rearrange("p a b -> p (a b)")[:, 0:1024].rearrange("p (a b) -> p a b", b=128)
                        ld(sv, wsT_d[:, :, :], W.rstg[i])
                        k.op("dve", lambda e, sv=sv: e.tensor_copy(out=wsTb[:], in_=sv), reads=[W.rstg[i]], writes=[rsm])
                        ld(lng[:], lng_d[:, :], rsm)
                        ld(bsT[:], bsT_d[:, :], rsm)
                    with ExitStack() as sA:
                        W = mk_work(sA, 2)
                        Wkv = sb(sA, [128, 8, 1024], BF16, "Wkv")
                        rW = Res("Wkv")
                        load_rows(Wkv, 0, 8, win_v, 512, 1024, W.stg, W.rstg, rW, W.cnt)
                        def frontA(t):
                            sl = t % 2
                            src = x_d[b * SEQ + t * 128:b * SEQ + (t + 1) * 128, :] if t < 16 else \
                                ctx_d[b * 256 + (t - 16) * 128:b * 256 + (t - 15) * 128, :]
                            ld(W.xin[sl][:], src, W.rxin[sl])
                            norm_front(W.xin[sl], W.rxin[sl], W, sl)

                        frontA(0)
                        norm_back(b, W, 0)
                        wgen = b1_weights(W)
                        for t in range(18):
                            sl = t % 2
                            W.sel(sl)
                            if t + 1 < 18:
                                frontA(t + 1)
                            next(wgen, None)
                            for hp in range(4):
                                for kb in range(8):
                                    k.op("pe", lambda e, hp=hp, kb=kb: e.matmul(
                                        v8(PB)[:, hp, :], lhsT=Wkv[:, kb, hp * 128:(hp + 1) * 128], rhs=W.hT[:, kb, :],
                                        start=(kb == 0), stop=(kb == 7)), reads=[rW, W.rhT], writes=[rPB])
                            k.op("act", lambda e, t=t: e.copy(out=KT[:, :, t * 128:(t + 1) * 128], in_=v8(PB)[:, 0:4, :]),
                                 reads=[rPB], writes=[rKT])
                            for kb in range(8):
                                k.op("pe", lambda e, kb=kb: e.matmul(
                                    PC[:, 0:512], lhsT=W.hT[:, kb, :], rhs=Wkv[:, kb, 512:1024],
                                    start=(kb == 0), stop=(kb == 7)), reads=[rW, W.rhT], writes=[rPC])
                            k.op("dve", lambda e, t=t: e.tensor_copy(
                                out=Vaug[:, t, :, 0:64], in_=PC[:, 0:512].rearrange("p (h d) -> p h d", d=64)),
                                reads=[rPC], writes=[rV])
                            if t + 1 < 18:
                                norm_back(b if t + 1 < 16 else 2, W, (t + 1) % 2)
                        for _ in wgen:
                            pass
                        k.barrier()
                    with ExitStack() as sB:
                        W = mk_work(sB)
                        qpad = sb(sB, [128, 8, 128], BF16, "qpad")
                        rq = Res("qpad")
                        u = sb(sB, [128, 512], F32, "u")
                        ru = Res("u")
                        gl = sb(sB, [128, 512], F32, "gl")
                        rgl = Res("gl")
                        st6 = sb(sB, [128, 8], F32, "st6")
                        rst = Res("st6")
                        vn = sb(sB, [128, 512], BF16, "vn")
                        rvn = Res("vn")
                        tt = sb(sB, [128, 512], F32, "tt")
                        rtt = Res("tt")
                        yab = sb(sB, [128, 1024], F32, "yab")
                        rya, ryb = Res("ya"), Res("yb")
                        PT = [sb(sB, [128, 8, 128], BF16, "PT") for _ in range(2)]
                        rPT = [Res("PT0"), Res("PT1")]
                        rcp = sb(sB, [128, 8], F32, "rcp")
                        rrcp = Res("rcp")
                        k.op("dve", lambda e: e.memset(qpad[:], 0.0), writes=[rq])
                        if b == 0 or True:
                            bcast(b, 2048)
                            k.op("act", lambda e: e.copy(out=G1[:], in_=PC[:, :]), reads=[rPC], writes=[rG1])
                        def frontB(cc):
                            sl = cc % 2
                            ld(W.xin[sl][:], x_d[b * SEQ + cc * 128:b * SEQ + (cc + 1) * 128, :], W.rxin[sl])
                            norm_front(W.xin[sl], W.rxin[sl], W, sl)

                        frontB(0)
                        norm_back(b, W, 0)
                        for cch in range(16):
                            sl = cch % 2
                            W.sel(sl)
                            if cch + 1 < 16:
                                frontB(cch + 1)
                            for hp in range(4):
                                for kb in range(8):
                                    k.op("pe", lambda e, hp=hp, kb=kb: e.matmul(
                                        v8(PB)[:, hp, :], lhsT=W3[:, kb, hp * 128:(hp + 1) * 128], rhs=W.hT[:, kb, :],
                                        start=(kb == 0), stop=(kb == 7)), reads=[rW3, W.rhT], writes=[rPB])
                            qv = qpad[:].rearrange("p (hp two) t -> p hp two t", two=2)
                            k.op("act", lambda e: e.mul(out=qv[0:64, :, 0, :], in_=v8(PB)[0:64, 0:4, :], mul=0.125),
                                 reads=[rPB], writes=[rq])
                            k.op("act", lambda e: e.mul(out=qv[64:128, :, 1, :], in_=v8(PB)[64:128, 0:4, :], mul=0.125),
                                 reads=[rPB], writes=[rq])
                            for kb in range(8):
                                k.op("pe", lambda e, kb=kb: e.matmul(
                                    PC[:, 0:512], lhsT=W.hT[:, kb, :], rhs=W3[:, kb, 512:1024],
                                    start=(kb == 0), stop=(kb == 7)), reads=[rW3, W.rhT], writes=[rPC])
                            k.op("act", lambda e: e.activation(out=u[:], in_=PC[:, 0:512], func=AF.Gelu_apprx_tanh),
                                 reads=[rPC], writes=[ru])
                            for kb in range(8):
                                k.op("pe", lambda e, kb=kb: e.matmul(
                                    PD[:, 0:512], lhsT=W.hT[:, kb, :], rhs=W3[:, kb, 1024:1536],
                                    start=(kb == 0), stop=(kb == 7)), reads=[rW3, W.rhT], writes=[rPD])
                            k.op("act", lambda e: e.activation(out=gl[:], in_=PD[:, 0:512], func=AF.Gelu_apprx_tanh),
                                 reads=[rPD], writes=[rgl])
                            k.op("dve", lambda e: e.bn_stats(out=st6[:, 0:6], in_=gl[:]), reads=[rgl], writes=[rst])
                            k.op("dve", lambda e: e.bn_aggr(out=st6[:, 6:8], in_=st6[:, 0:6]), reads=[rst], writes=[rst])
                            k.op("act", lambda e: e.activation(out=st6[:, 0:1], in_=st6[:, 7:8], func=AF.Sqrt,
                                                               scale=1.0, bias=epsb[:, 0:1]),
                                 reads=[rst, rconst], writes=[rst])
                            k.op("dve", lambda e: e.reciprocal(out=st6[:, 1:2], in_=st6[:, 0:1]), reads=[rst], writes=[rst])
                            k.op("dve", lambda e: e.tensor_scalar(out=vn[:], in0=gl[:], scalar1=st6[:, 6:7],
                                                                  scalar2=st6[:, 1:2], op0=ALU.subtract, op1=ALU.mult),
                                 reads=[rgl, rst], writes=[rvn])
                            if cch + 1 < 16:
                                norm_back(b, W, (cch + 1) % 2)
                            for g in range(8):
                                k.op("pe", lambda e, g=g: e.matmul(
                                    PD[:, 512 + g * 64:512 + (g + 1) * 64], lhsT=wsTb[:, g, :],
                                    rhs=vn[:, g * 64:(g + 1) * 64], start=True, stop=True),
                                    reads=[rsm, rvn], writes=[rPD])
                            k.op("dve", lambda e: e.tensor_tensor(out=tt[:], in0=PD[:, 512:1024], in1=lng[:], op=ALU.mult),
                                 reads=[rPD, rsm], writes=[rtt])
                            ttv = tt[:].rearrange("p (g d) -> p g d", d=64)
                            k.op("dve", lambda e: e.tensor_tensor(out=ttv, in0=ttv,
                                                                  in1=bsT[:].unsqueeze(2).to_broadcast([128, 8, 64]),
                                                                  op=ALU.add), reads=[rtt, rsm], writes=[rtt])
                            k.op("dve", lambda e: e.tensor_tensor(out=yab[:, 512:1024], in0=tt[:], in1=u[:], op=ALU.mult),
                                 reads=[rtt, ru], writes=[ryb])
                            klist = CHUNK_KEYS[cch]
                            nb = len(klist)
                            nj = nb + 2
                            def att_S(h):
                                hp = h // 2
                                S, rS = (PC, rPC) if h % 2 == 0 else (PD, rPD)
                                Sv = v8(S)
                                for j, (kc, ti) in enumerate(klist):
                                    k.op("pe", lambda e, j=j, kc=kc, Sv=Sv, hp=hp, h=h: e.matmul(
                                        Sv[:, j, :], lhsT=KT[:, hp, kc * 128:(kc + 1) * 128], rhs=qpad[:, h, :],
                                        start=True, stop=False), reads=[rKT, rq], writes=[rS])
                                    k.op("pe", lambda e, j=j, ti=ti, Sv=Sv, h=h: e.matmul(
                                        Sv[:, j, :], lhsT=biasb[:, h * NTT + ti, :], rhs=identb[:],
                                        start=False, stop=True), reads=[rbias, rconst], writes=[rS])
                                for c in range(2):
                                    k.op("pe", lambda e, c=c, Sv=Sv, hp=hp, h=h: e.matmul(
                                        Sv[:, nb + c, :], lhsT=KT[:, hp, 2048 + c * 128:2048 + (c + 1) * 128],
                                        rhs=qpad[:, h, :], start=True, stop=True), reads=[rKT, rq], writes=[rS])
                                pt, rpt = PT[h % 2], rPT[h % 2]
                                k.op("act", lambda e, pt=pt, Sv=Sv: e.activation(out=pt[:, 0:nj, :], in_=Sv[:, 0:nj, :],
                                                                                 func=AF.Exp),
                                     reads=[rS], writes=[rpt])

                            def att_PV(h):
                                pt, rpt = PT[h % 2], rPT[h % 2]
                                ocol = (h // 4) * 512 + (h % 4) * 65
                                for j in range(nj):
                                    vt = klist[j][0] if j < nb else 16 + (j - nb)
                                    k.op("pe", lambda e, j=j, vt=vt, pt=pt, h=h, ocol=ocol: e.matmul(
                                        PB[:, ocol:ocol + 65], lhsT=pt[:, j, :], rhs=Vaug[:, vt, h, :],
                                        start=(j == 0), stop=(j == nj - 1)), reads=[rpt, rV], writes=[rPB])

                            for h in range(9):
                                if h < 8:
                                    att_S(h)
                                if h >= 1:
                                    att_PV(h - 1)
                            for a in range(2):
                                Ov = PB[:, a * 512:a * 512 + 260].rearrange("p (h e) -> p h e", e=65)
                                k.op("dve", lambda e, a=a, Ov=Ov: e.reciprocal(out=rcp[:, a * 4:(a + 1) * 4], in_=Ov[:, :, 64]),
                                     reads=[rPB], writes=[rrcp])
                                k.op("dve", lambda e, a=a, Ov=Ov: e.tensor_tensor(
                                    out=yab[:, a * 256:(a + 1) * 256].rearrange("p (h d) -> p h d", d=64),
                                    in0=Ov[:, :, 0:64],
                                    in1=rcp[:, a * 4:(a + 1) * 4].unsqueeze(2).to_broadcast([128, 4, 64]), op=ALU.mult),
                                    reads=[rPB, rrcp], writes=[rya])
                            for blk in range(8):
                                k.op("pe", lambda e, blk=blk: e.transpose(out=v8(PD)[:, blk, :],
                                                                          in_=yab[:, blk * 128:(blk + 1) * 128],
                                                                          identity=ident[:]),
                                     reads=[rya, ryb, rconst], writes=[rPD])
                            k.op("act", lambda e, cch=cch: e.copy(out=YT[:, cch, :, :], in_=v8(PD)[:, :, :]),
                                 reads=[rPD], writes=[rYT])
                        k.barrier()
                with ExitStack() as sC:
                    W = mk_work(sC, 4)
                    Wg = sb(sC, [128, 8, 2048], BF16, "Wg")
                    Wpa = sb(sC, [128, 4, 1024], BF16, "Wpa")
                    Wpb = sb(sC, [128, 4, 1024], BF16, "Wpb")
                    Wo = sb(sC, [128, 8, 1024], BF16, "Wo")
                    rWg, rWp, rWo = Res("Wg"), Res("Wp"), Res("Wo")
                    sga = sb(sC, [128, 8, 128], F32, "sga")
                    sgb = sb(sC, [128, 8, 128], F32, "sgb")
                    rsga, rsgb = Res("sga"), Res("sgb")
                    t1 = sb(sC, [128, 1024], F32, "t1")
                    t2 = sb(sC, [128, 1024], F32, "t2")
                    rt1, rt2 = Res("t1"), Res("t2")
                    mT = sb(sC, [128, 8, 128], BF16, "mT")
                    rmT = Res("mT")
                    x1 = [sb(sC, [128, 1024], F32, "x1") for _ in range(2)]
                    rx1 = [Res("x1a"), Res("x1b")]
                    load_rows(Wg, 0, 8, win_v, 2560, 2048, W.stg, W.rstg, rWg, W.cnt)
                    wout_v = wout_d.rearrange("(kb p) c -> p kb c", p=128)
                    load_rows(Wpa, 0, 4, wpa_d.rearrange("(kb p) c -> p kb c", p=128), 0, 1024, W.stg, W.rstg, rWp, W.cnt)
                    load_rows(Wpb, 0, 4, wpb_d.rearrange("(kb p) c -> p kb c", p=128), 0, 1024, W.stg, W.rstg, rWp, W.cnt)
                    load_rows(Wo, 0, 8, wout_v, 0, 1024, W.stg, W.rstg, rWo, W.cnt)
                    def frontC(cc):
                        sl = cc % 2
                        ld(W.xin[sl][:], x_d[b * SEQ + cc * 128:b * SEQ + (cc + 1) * 128, :], W.rxin[sl])
                        norm_front(W.xin[sl], W.rxin[sl], W, sl)

                    frontC(0)
                    norm_back(b, W, 0)
                    for cch in range(16):
                        sl = cch % 2
                        r0 = b * SEQ + cch * 128
                        W.sel(sl)
                        if cch + 1 < 16:
                            frontC(cch + 1)
                        for (c_off, P_, rP_, sg, rsg) in ((0, PB, rPB, sga, rsga), (1024, PC, rPC, sgb, rsgb)):
                            for ob in range(8):
                                for kb in range(8):
                                    k.op("pe", lambda e, ob=ob, kb=kb, P_=P_, c_off=c_off: e.matmul(
                                        v8(P_)[:, ob, :], lhsT=Wg[:, kb, c_off + ob * 128:c_off + (ob + 1) * 128],
                                        rhs=W.hT[:, kb, :], start=(kb == 0), stop=(kb == 7)),
                                        reads=[rWg, W.rhT], writes=[rP_])
                            k.op("act", lambda e, P_=P_, sg=sg: e.activation(out=sg[:], in_=v8(P_)[:, :, :], func=AF.Sigmoid),
                                 reads=[rP_], writes=[rsg])
                        if cch + 1 < 16:
                            norm_back(b, W, (cch + 1) % 2)
                        for (y0, P_, rP_, wt) in ((0, PD, rPD, Wpa), (4, PB, rPB, Wpb)):
                            for ob in range(8):
                                for kb in range(4):
                                    k.op("pe", lambda e, ob=ob, kb=kb, P_=P_, wt=wt, y0=y0: e.matmul(
                                        v8(P_)[:, ob, :], lhsT=wt[:, kb, ob * 128:(ob + 1) * 128],
                                        rhs=YT[:, cch, y0 + kb, :], start=(kb == 0), stop=(kb == 3)),
                                        reads=[rWp, rYT], writes=[rP_])
                        k.op("dve", lambda e: e.tensor_tensor(out=t1[:], in0=PD[:, :], in1=sga[:].rearrange("p a b -> p (a b)"),
                                                              op=ALU.mult), reads=[rPD, rsga], writes=[rt1])
                        k.op("dve", lambda e: e.tensor_tensor(out=t2[:], in0=PB[:, :], in1=sgb[:].rearrange("p a b -> p (a b)"),
                                                              op=ALU.mult), reads=[rPB, rsgb], writes=[rt2])
                        k.op("dve", lambda e: e.tensor_tensor(out=mT[:].rearrange("p a b -> p (a b)"), in0=t1[:], in1=t2[:],
                                                               op=ALU.add), reads=[rt1, rt2], writes=[rmT])
                        for half in range(2):
                            for kb in range(8):
                                k.op("pe", lambda e, half=half, kb=kb: e.matmul(
                                    PC[:, half * 512:(half + 1) * 512], lhsT=mT[:, kb, :],
                                    rhs=Wo[:, kb, half * 512:(half + 1) * 512], start=(kb == 0), stop=(kb == 7)),
                                    reads=[rmT, rWo], writes=[rPC])
                        k.op("dve", lambda e: e.tensor_tensor(out=t1[:], in0=PC[:, :], in1=G1[:], op=ALU.mult),
                             reads=[rPC, rG1], writes=[rt1])
                        k.op("dve", lambda e, sl=sl: e.tensor_tensor(out=x1[sl][:], in0=t1[:], in1=W.xin[sl][:], op=ALU.add),
                             reads=[rt1, W.rxin[sl]], writes=[rx1[sl]])
                        k.dma("sp", lambda e, sl=sl, r0=r0: e.dma_start(out=out_d[r0:r0 + 128, :], in_=x1[sl][:]),
                              rx1[sl], reads=[rx1[sl]])
                    k.barrier()
            k.barrier(full=True)

        GD = BF16
        NBUF = 12
        GRP = 4
        PF = NBUF // GRP - 1
        G_INS = 2
        with ExitStack() as sP:
            wq = sb(sP, [128, 8, 2048], BF16, "wq")
            keysT = sb(sP, [128, 16, 128], BF16, "keysT")
            rwq, rkeys = Res("wq"), Res("keys")
            fg = sb(sP, [128, D], F32, "fg")
            iota = sb(sP, [128, 16], F32, "iota")
            rpc = Res("pconst")
            _A2 = sb(sP, [128, D], F32, "A2")
            _B2 = sb(sP, [128, D], F32, "B2")
            A2 = [_A2, _A2]
            B2 = [_B2, _B2]
            G2 = [sb(sP, [128, D], F32, "G2") for _ in range(2)]
            rmod2 = Res("mod2")
            cnt = [0]
            with ExitStack() as sPs:
                wq_v = wq_d.rearrange("(kb p) c -> p kb c", p=128)
                stg = [sb(sPs, [128, 8, 256], F32, "pstg") for _ in range(4)]
                rstg = [Res("pstg%d" % i) for i in range(4)]
                n2g = sb(sPs, [128, D], F32, "n2g")
                load_rows(wq, 0, 8, wq_v, 0, 2048, stg, rstg, rwq, cnt)
                i = cnt[0] % 4
                cnt[0] += 1
                sv = stg[i][:].rearrange("p a b -> p (a b)").rearrange("p (a b) -> p a b", b=128)
                ld(sv, keysT_d[:, :, :], rstg[i])
                k.op("dve", lambda e, sv=sv: e.tensor_copy(out=keysT[:], in_=sv), reads=[rstg[i]], writes=[rkeys])
                ld(n2g[:], n2g_d[:, :], rpc)
                ld(fg[:], fg_d[:, :], rpc)
                ld(iota[:], iota_d[:, :], rpc)
                rg2 = Res("g2")
                bcast(0, 4096)
                k.op("dve", lambda e: e.scalar_tensor_tensor(out=A2[0][:], in0=PC[:, :], scalar=1.0, in1=n2g[:],
                                                             op0=ALU.add, op1=ALU.mult),
                     reads=[rPC, rpc], writes=[rmod2])
                bcast(0, 3072)
                k.op("act", lambda e: e.copy(out=B2[0][:], in_=PC[:, :]), reads=[rPC], writes=[rmod2])
                for b in range(2):
                    bcast(b, 5120)
                    k.op("act", lambda e, b=b: e.copy(out=G2[b][:], in_=PC[:, :]), reads=[rPC], writes=[rg2])
                k.barrier()
            x1t = [sb(sP, [128, D], F32, "x1t") for _ in range(2)]
            rx1t = [Res("x1t0"), Res("x1t1")]
            h2 = [sb(sP, [128, D], F32, "h2") for _ in range(2)]
            rh2 = [Res("h20"), Res("h21")]
            junkb = sb(sP, [128, D], BF16, "pjunkb")
            rjunkb = Res("pjunkb")
            junkd = sb(sP, [128, D], BF16, "pjunkd")
            rjunkd = Res("pjunkd")
            ssq = sb(sP, [128, 8], F32, "pssq")
            rssq = Res("pssq")
            ssq3 = sb(sP, [128, 8], F32, "pssq3")
            rssq3 = Res("pssq3")
            h2T = sb(sP, [128, 8, 128], BF16, "h2T")
            rh2T = Res("h2T")
            qT = sb(sP, [128, 16, 128], BF16, "qT")
            rqT = Res("qT")
            W0 = sb(sP, [128, 2048], F32, "W0")
            W1 = sb(sP, [128, 2048], F32, "W1")
            rW0, rW1 = Res("W0"), Res("W1")
            top1 = sb(sP, [128, 16, 16], F32, "top1")
            idx1 = sb(sP, [128, 16, 16], U32, "idx1")
            idx1f = sb(sP, [128, 16, 16], F32, "idx1f")
            rtop1, ridx1 = Res("top1"), Res("idx1")
            best = sb(sP, [128, 8, 16], F32, "best")
            pos = sb(sP, [128, 8, 16], U32, "pos")
            rbest, rpos = Res("best"), Res("pos")
            pa_u = sb(sP, [128, 8, 16], U32, "pa_u")
            pb_u = sb(sP, [128, 8, 16], U32, "pb_u")
            pa_f = sb(sP, [128, 8, 16], F32, "pa_f")
            pb_f = sb(sP, [128, 8, 16], F32, "pb_f")
            i0f = sb(sP, [128, 8, 16], F32, "i0f")
            i1f = sb(sP, [128, 8, 16], F32, "i1f")
            rdec = Res("dec")
            ef = sb(sP, [128, 128], F32, "ef")
            ei = [sb(sP, [128, 128], I32, "ei") for _ in range(2)]
            rei = [Res("ei0"), Res("ei1")]
            gate = [sb(sP, [128, 128], F32, "gate") for _ in range(2)]
            rgate = [Res("gate0"), Res("gate1")]
            gtmp = sb(sP, [128, 8, 16], F32, "gtmp")
            gsum = sb(sP, [128, 16], F32, "gsum")
            rgt = Res("gtmp")
            NGR = 128 // GRP
            actv = sb(sP, [128, 128], F32, "actv")
            gel = sb(sP, [128, 128], F32, "gel")
            wgt = sb(sP, [128, 128], F32, "wgt")
            ract = [Res("act%d" % g) for g in range(NGR)]
            rgel = [Res("gel%d" % g) for g in range(NGR)]
            rwgt = [Res("wgt%d" % g) for g in range(NGR)]
            Gb = [sb(sP, [128, 2 * D], GD, "Gb") for _ in range(NBUF)]
            rGb = [Res("Gb%d" % i) for i in range(NBUF)]
            NDG = 8
            dg = [sb(sP, [128, 128], GD, "dg") for _ in range(NDG)]
            rdg = [Res("dg%d" % i) for i in range(NDG)]
            identg = ident if GD == F32 else identb
            x2 = W0
            rx2 = rW0
            t3 = W1
            rt3 = rW1

            def topk16(src3, rsrc, dst_top, dst_idx, rtop, ridx, scratch3, rscr, ngrp):
                for g in range(ngrp):
                    k.op("dve", lambda e, g=g: e.max(out=dst_top[:, g, 0:8], in_=src3[:, g, :]), reads=[rsrc], writes=[rtop])
                    if g % 8 == 7:
                        yield
                for g in range(ngrp):
                    k.op("dve", lambda e, g=g: e.max_index(out=dst_idx[:, g, 0:8], in_max=dst_top[:, g, 0:8],
                                                          in_values=src3[:, g, :]), reads=[rsrc, rtop], writes=[ridx])
                    if g % 8 == 7:
                        yield
                for g in range(ngrp):
                    k.op("dve", lambda e, g=g: e.match_replace(out=scratch3[:, g, :], in_to_replace=dst_top[:, g, 0:8],
                                                              in_values=src3[:, g, :], imm_value=-1e30),
                         reads=[rsrc, rtop], writes=[rscr])
                    if g % 8 == 7:
                        yield
                for g in range(ngrp):
                    k.op("dve", lambda e, g=g: e.max(out=dst_top[:, g, 8:16], in_=scratch3[:, g, :]),
                         reads=[rscr], writes=[rtop])
                    if g % 8 == 7:
                        yield
                for g in range(ngrp):
                    k.op("dve", lambda e, g=g: e.max_index(out=dst_idx[:, g, 8:16], in_max=dst_top[:, g, 8:16],
                                                          in_values=scratch3[:, g, :]), reads=[rscr, rtop], writes=[ridx])
                    if g % 8 == 7:
                        yield

            def stage1(ci):
                b = ci // 16
                sl = ci % 2
                r0 = ci * 128
                if ci == 16:
                    ld(W1[:, 0:D], n2g_d[:, :], rW1)
                    bcast(1, 4096)
                    k.op("dve", lambda e: e.scalar_tensor_tensor(out=A2[1][:], in0=PC[:, :], scalar=1.0, in1=W1[:, 0:D],
                                                                 op0=ALU.add, op1=ALU.mult),
                         reads=[rPC, rW1], writes=[rmod2])
                    bcast(1, 3072)
                    k.op("act", lambda e: e.copy(out=B2[1][:], in_=PC[:, :]), reads=[rPC], writes=[rmod2])
                ld(x1t[sl][:], out_d[r0:r0 + 128, :], rx1t[sl])
                k.op("act", lambda e: e.activation(out=junkb[:], in_=x1t[sl][:], func=AF.Square, accum_out=ssq[:, 0:1]),
                     reads=[rx1t[sl]], writes=[rjunkb, rssq])
                k.op("act", lambda e: e.activation(out=ssq[:, 1:2], in_=ssq[:, 0:1], func=AF.Sqrt, scale=1.0 / D,
                                                   bias=epsb[:, 0:1]), reads=[rssq, rconst], writes=[rssq])
                k.op("dve", lambda e: e.reciprocal(out=ssq[:, 2:3], in_=ssq[:, 1:2]), reads=[rssq], writes=[rssq])
                k.op("dve", lambda e: e.scalar_tensor_tensor(out=h2[sl][:], in0=x1t[sl][:], scalar=ssq[:, 2:3],
                                                             in1=A2[b][:], op0=ALU.mult, op1=ALU.mult),
                     reads=[rx1t[sl], rssq, rmod2], writes=[rh2[sl]])
                k.op("dve", lambda e: e.tensor_tensor(out=h2[sl][:], in0=h2[sl][:], in1=B2[b][:], op=ALU.add),
                     reads=[rh2[sl], rmod2], writes=[rh2[sl]])
                for blk in range(8):
                    k.op("pe", lambda e, blk=blk: e.transpose(out=v8(PA)[:, blk, :],
                                                              in_=h2[sl][:, blk * 128:(blk + 1) * 128],
                                                              identity=ident[:]),
                         reads=[rh2[sl], rconst], writes=[rPA])
                k.op("act", lambda e: e.copy(out=h2T[:], in_=v8(PA)[:, :, :]), reads=[rPA], writes=[rh2T])
                yield
                for half, (P_, rP_) in enumerate(((PB, rPB), (PC, rPC))):
                    for ob in range(8):
                        gi = half * 8 + ob
                        for kb in range(8):
                            k.op("pe", lambda e, ob=ob, gi=gi, kb=kb, P_=P_: e.matmul(
                                v8(P_)[:, ob, :], lhsT=wq[:, kb, gi * 128:(gi + 1) * 128], rhs=h2T[:, kb, :],
                                start=(kb == 0), stop=(kb == 7)), reads=[rwq, rh2T], writes=[rP_])
                    k.op("act", lambda e, half=half, P_=P_: e.copy(out=qT[:, half * 8:(half + 1) * 8, :], in_=v8(P_)[:, :, :]),
                         reads=[rP_], writes=[rqT])
                    yield
                s3 = W0[:].rearrange("p (g k) -> p g k", k=128)
                s3b = W1[:].rearrange("p (g k) -> p g k", k=128)
                for half, (P_, rP_) in enumerate(((PA, rPA), (PB, rPB))):
                    for ob in range(8):
                        gi = half * 8 + ob
                        k.op("pe", lambda e, ob=ob, gi=gi, P_=P_: e.matmul(
                            v8(P_)[:, ob, :], lhsT=qT[:, gi, :], rhs=keysT[:, gi, :], start=True, stop=True),
                            reads=[rqT, rkeys], writes=[rP_])
                    k.op("act", lambda e, half=half, P_=P_: e.copy(out=s3[:, half * 8:(half + 1) * 8, :], in_=v8(P_)[:, :, :]),
                         reads=[rP_], writes=[rW0])
                yield
                yield from topk16(s3, rW0, top1, idx1, rtop1, ridx1, s3b, rW1, 16)
                t1v = top1[:].rearrange("p (h two) k -> p h two k", two=2)
                cand = W0[:].rearrange("p (h a b) -> p h a b", a=16, b=16)
                k.op("dve", lambda e: e.tensor_tensor(
                    out=cand, in0=t1v[:, :, 0, :].unsqueeze(3).to_broadcast([128, 8, 16, 16]),
                    in1=t1v[:, :, 1, :].unsqueeze(2).to_broadcast([128, 8, 16, 16]), op=ALU.add),
                    reads=[rtop1], writes=[rW0])
                c3 = W0[:].rearrange("p (h c) -> p h c", c=256)
                c3b = W1[:].rearrange("p (h c) -> p h c", c=256)
                yield
                yield from topk16(c3, rW0, best, pos, rbest, rpos, c3b, rW1, 8)
                k.op("dve", lambda e: e.tensor_single_scalar(out=pa_u[:], in_=pos[:], scalar=4, op=ALU.arith_shift_right),
                     reads=[rpos], writes=[rdec])
                k.op("dve", lambda e: e.tensor_single_scalar(out=pb_u[:], in_=pos[:], scalar=15, op=ALU.bitwise_and),
                     reads=[rpos], writes=[rdec])
                k.op("dve", lambda e: e.tensor_copy(out=pa_f[:], in_=pa_u[:]), reads=[rdec], writes=[rdec])
                k.op("dve", lambda e: e.tensor_copy(out=pb_f[:], in_=pb_u[:]), reads=[rdec], writes=[rdec])
                k.op("dve", lambda e: e.tensor_copy(out=idx1f[:], in_=idx1[:]), reads=[ridx1], writes=[ridx1])
                yield
                i1v = idx1f[:].rearrange("p (h two) k -> p h two k", two=2)
                oh = W1[:].rearrange("p (h a b) -> p h a b", a=16, b=16)
                for (pf, half, dst) in ((pa_f, 0, i0f), (pb_f, 1, i1f)):
                    k.op("dve", lambda e, pf=pf: e.tensor_tensor(
                        out=oh, in0=pf[:].unsqueeze(3).to_broadcast([128, 8, 16, 16]),
                        in1=iota[:].unsqueeze(1).unsqueeze(1).to_broadcast([128, 8, 16, 16]), op=ALU.is_equal),
                        reads=[rdec, rpc], writes=[rW1])
                    k.op("dve", lambda e, half=half: e.tensor_tensor(
                        out=oh, in0=oh, in1=i1v[:, :, half, :].unsqueeze(2).to_broadcast([128, 8, 16, 16]), op=ALU.mult),
                        reads=[rW1, ridx1], writes=[rW1])
                    k.op("dve", lambda e, dst=dst: e.tensor_reduce(out=dst[:], in_=oh, axis=AX.X, op=ALU.add),
                         reads=[rW1], writes=[rdec])
                    yield
                k.op("dve", lambda e: e.scalar_tensor_tensor(
                    out=ef[:], in0=i0f[:].rearrange("p h k -> p (h k)"), scalar=128.0,
                    in1=i1f[:].rearrange("p h k -> p (h k)"), op0=ALU.mult, op1=ALU.add), reads=[rdec], writes=[rdec])
                k.op("dve", lambda e: e.tensor_copy(out=ei[sl][:], in_=ef[:]), reads=[rdec], writes=[rei[sl]])
                k.op("dve", lambda e: e.tensor_tensor(out=gtmp[:], in0=best[:],
                                                      in1=best[:, :, 0:1].to_broadcast([128, 8, 16]), op=ALU.subtract),
                     reads=[rbest], writes=[rgt])
                k.op("act", lambda e: e.activation(out=gtmp[:], in_=gtmp[:], func=AF.Exp), reads=[rgt], writes=[rgt])
                k.op("dve", lambda e: e.tensor_reduce(out=gsum[:, 0:8], in_=gtmp[:], axis=AX.X, op=ALU.add),
                     reads=[rgt], writes=[rgt])
                k.op("dve", lambda e: e.reciprocal(out=gsum[:, 8:16], in_=gsum[:, 0:8]), reads=[rgt], writes=[rgt])
                k.op("dve", lambda e: e.tensor_tensor(out=gate[sl][:].rearrange("p (h k) -> p h k", k=16), in0=gtmp[:],
                                                      in1=gsum[:, 8:16].unsqueeze(2).to_broadcast([128, 8, 16]),
                                                      op=ALU.mult), reads=[rgt], writes=[rgate[sl]])

            def gathers(gg):
                ci, g = divmod(gg, NGR)
                sl = ci % 2
                for kk in range(GRP):
                    s = g * GRP + kk
                    bi = (gg * GRP + kk) % NBUF
                    k.dma("pool", lambda e, s=s, bi=bi: e.indirect_dma_start(
                        out=Gb[bi][:], out_offset=None, in_=uv_src[:, :],
                        in_offset=bass.IndirectOffsetOnAxis(ap=ei[sl][:, s:s + 1], axis=0)),
                        rGb[bi], reads=[rei[sl]], writes=[rGb[bi]])

            def compute_a(gg):
                ci, g = divmod(gg, NGR)
                sl = ci % 2
                c0, c1 = g * GRP, (g + 1) * GRP
                for kk in range(GRP):
                    s = g * GRP + kk
                    bi = (gg * GRP + kk) % NBUF
                    k.op("dve", lambda e, s=s, bi=bi: e.scalar_tensor_tensor(
                        out=junkd[:], in0=Gb[bi][:, 0:D], scalar=1.0, in1=h2[sl][:], op0=ALU.mult, op1=ALU.mult,
                        accum_out=actv[:, s:s + 1]), reads=[rGb[bi], rh2[sl]], writes=[rjunkd, ract[g]])
                k.op("act", lambda e: e.activation(out=gel[:, c0:c1], in_=actv[:, c0:c1], func=AF.Gelu_apprx_tanh),
                     reads=[ract[g]], writes=[rgel[g]])

            def compute_b(gg):
                ci, g = divmod(gg, NGR)
                sl = ci % 2
                c0, c1 = g * GRP, (g + 1) * GRP
                k.op("dve", lambda e: e.tensor_tensor(out=wgt[:, c0:c1], in0=gel[:, c0:c1], in1=gate[sl][:, c0:c1],
                                                      op=ALU.mult), reads=[rgel[g], rgate[sl]], writes=[rwgt[g]])
                for kk in range(GRP):
                    s = g * GRP + kk
                    bi = (gg * GRP + kk) % NBUF
                    di = (gg * GRP + kk) % NDG
                    k.op("act", lambda e, s=s, di=di: e.activation(out=dg[di][:], in_=identg[:], func=AF.Copy,
                                                                  scale=wgt[:, s:s + 1]),
                         reads=[rwgt[g], rconst], writes=[rdg[di]])
                    for half in range(2):
                        k.op("pe", lambda e, s=s, bi=bi, di=di, half=half: e.matmul(
                            PD[:, half * 512:(half + 1) * 512], lhsT=dg[di][:],
                            rhs=Gb[bi][:, D + half * 512:D + (half + 1) * 512],
                            start=(s == 0), stop=(s == 127)), reads=[rdg[di], rGb[bi]], writes=[rPD])

            def finalize(ci):
                b = ci // 16
                sl = ci % 2
                r0 = ci * 128
                k.op("dve", lambda e: e.tensor_tensor(out=t3[:, 0:D], in0=PD[:, :], in1=G2[b][:], op=ALU.mult),
                     reads=[rPD, rg2], writes=[rt3])
                k.op("dve", lambda e: e.tensor_tensor(out=x2[:, 0:D], in0=t3[:, 0:D], in1=x1t[sl][:], op=ALU.add),
                     reads=[rt3, rx1t[sl]], writes=[rx2])
                k.op("act", lambda e: e.activation(out=junkb[:], in_=x2[:, 0:D], func=AF.Square, accum_out=ssq3[:, 0:1]),
                     reads=[rx2], writes=[rjunkb, rssq3])
                k.op("act", lambda e: e.activation(out=ssq3[:, 1:2], in_=ssq3[:, 0:1], func=AF.Sqrt, scale=1.0 / D,
                                                   bias=epsb[:, 0:1]), reads=[rssq3, rconst], writes=[rssq3])
                k.op("dve", lambda e: e.reciprocal(out=ssq3[:, 2:3], in_=ssq3[:, 1:2]), reads=[rssq3], writes=[rssq3])
                k.op("dve", lambda e: e.scalar_tensor_tensor(out=x2[:, 0:D], in0=x2[:, 0:D], scalar=ssq3[:, 2:3], in1=fg[:],
                                                             op0=ALU.mult, op1=ALU.mult),
                     reads=[rx2, rssq3, rpc], writes=[rx2])
                k.dma("sp", lambda e: e.dma_start(out=out_d[r0:r0 + 128, :], in_=x2[:, 0:D]), rx2, reads=[rx2])

            uv_src = uvb_d
            NCH = 32
            TOT = NCH * NGR
            for _ in stage1(0):
                pass
            for gg in range(min(PF, TOT)):
                gathers(gg)
            gen = None
            for gg in range(TOT):
                ci, g = divmod(gg, NGR)
                if g == G_INS and ci + 1 < NCH:
                    gen = stage1(ci + 1)
                if gg + PF < TOT:
                    gathers(gg + PF)
                compute_a(gg)
                if gen is not None:
                    if g >= NGR - PF - 1:
                        for _ in gen:
                            pass
                        gen = None
                    elif next(gen, "done") == "done":
                        gen = None
                compute_b(gg)
                if g == NGR - 1:
                    finalize(ci)
            k.barrier()
    return nc


_NC_CACHE = {}


def kernel(x, c, ctx, c_ctx, ada_w, ada_b, norm1_g, norm2_g, w_in, na_rpb, gm_ln_g, gm_ws, gm_bs,
           w_proj_a, w_proj_b, w_out, peer_wq, peer_keys, peer_u, peer_v, final_g):
    f = lambda a: np.ascontiguousarray(np.asarray(a, dtype=np.float32))
    x, c, ctx, c_ctx = f(x), f(c), f(ctx), f(c_ctx)
    if "nc" not in _NC_CACHE:
        _NC_CACHE["nc"] = build_program()
    nc = _NC_CACHE["nc"]
    shared = {
        "ada_w": f(ada_w)[0],
        "adab_rep": f(np.broadcast_to(f(ada_b)[0][None, :], (4, 6 * D))),
        "n1gT": f(f(norm1_g)[0].reshape(8, 128).T),
        "n2g_rep": f(np.broadcast_to(f(norm2_g)[0][None, :], (128, D))),
        "fg_rep": f(np.broadcast_to(f(final_g)[None, :], (128, D))),
        "lng_rep": f(np.broadcast_to(f(gm_ln_g)[0][None, :], (128, 512))),
        "w_in": f(w_in)[0],
        "bias_t": _bias_tiles(f(na_rpb)[0]),
        "wsT": f(np.transpose(f(gm_ws)[0], (2, 0, 1))),
        "bsT": f(f(gm_bs)[0].T),
        "w_pa": f(w_proj_a)[0],
        "w_pb": f(w_proj_b)[0],
        "w_out": f(w_out)[0],
        "peer_wq": f(peer_wq)[0],
        "keysT": f(np.transpose(f(peer_keys)[0].reshape(16, 128, 128), (2, 0, 1))),
        "peer_uv": f(np.concatenate([f(peer_u)[0], f(peer_v)[0]], axis=1)),
        "ident": np.eye(128, dtype=np.float32),
        "iota16": f(np.broadcast_to(np.arange(16, dtype=np.float32)[None, :], (128, 16))),
        "sel": f(np.stack([np.stack([np.full(128, 1.0 if kk == bb else 0.0, np.float32) for bb in range(2)])
                           for kk in range(4)])),
    }
    in_maps = []
    for core in range(NCORES):
        b0 = 2 * core
        cv = np.stack([c[b0], c[b0 + 1], c_ctx, c_ctx], axis=1)
        m = dict(shared)
        m["x"] = f(x[b0:b0 + 2].reshape(2 * SEQ, D))
        m["ctx"] = f(ctx[b0:b0 + 2].reshape(512, D))
        m["cT"] = f(cv.reshape(8, 128, 4).transpose(1, 0, 2))
        in_maps.append(m)
    res = run_bass_kernel_spmd(nc, in_maps, core_ids=list(range(NCORES)))
    outs = [np.asarray(r["out"], dtype=np.float32).reshape(2, SEQ, D) for r in res.results]
    return np.concatenate(outs, axis=0)
```
